# Optimizing a Trainium2 kernel written in Bass

```python
import math
import jax, jax.numpy as jnp
from jax import lax
import numpy as np

D_MODEL = 4096
BATCH = 4
SEQ = 2048
DEPTH = 1

SGU_WIDTH = D_MODEL
SGU_GROUPS = 16
SGU_GROUP_DIM = SGU_WIDTH // SGU_GROUPS
CHUNK = 128
DIFF_HEADS = 16
DIFF_HEAD_DIM = D_MODEL // (2 * DIFF_HEADS)
DIFF_V_DIM = 2 * DIFF_HEAD_DIM
DIFF_QK_WIDTH = DIFF_HEADS * 2 * DIFF_HEAD_DIM
DIFF_WIDTH = DIFF_HEADS * DIFF_V_DIM
Q_BLOCK = 128
D_FF = 11008
CONV_WIDTH = 3
EPS = 1e-6
NEG_INF = -1e30
IN_SPLITS = (SGU_WIDTH, SGU_WIDTH, DIFF_QK_WIDTH, DIFF_QK_WIDTH, DIFF_WIDTH, SGU_WIDTH, DIFF_WIDTH)
IN_WIDTH = sum(IN_SPLITS)

kernel_name = "hybrid_sgu_diffattn_convffn_adaln"


def _rms(x, g):
    xf = x.astype(jnp.float32)
    y = xf * lax.rsqrt(jnp.mean(xf * xf, axis=-1, keepdims=True) + EPS)
    return (y * g.astype(jnp.float32)).astype(x.dtype)


def _lambda_init(layer_idx):
    return 0.8 - 0.6 * math.exp(-0.3 * layer_idx)


def _spatial_gating(u, v, sgu_norm_g, w_spatial, b_spatial):
    B, S, _ = u.shape
    v = _rms(v, sgu_norm_g)
    vc = v.reshape(B, S // CHUNK, CHUNK, SGU_GROUPS, SGU_GROUP_DIM)
    causal = jnp.tril(jnp.ones((CHUNK, CHUNK), dtype=bool))
    ws = jnp.where(causal[None], w_spatial, jnp.zeros_like(w_spatial))
    mixed = jnp.einsum('gts,bcsge->bctge', ws, vc) + b_spatial.T[:, :, None]
    return u * mixed.reshape(B, S, SGU_WIDTH)


def _diff_attention(q, k, v, q_norm_g, k_norm_g, lq1, lk1, lq2, lk2, subln_g, lambda_init):
    B, S, _ = q.shape
    H, d = DIFF_HEADS, DIFF_HEAD_DIM
    q = _rms(q.reshape(B, S, H, 2, d), q_norm_g)
    k = _rms(k.reshape(B, S, H, 2, d), k_norm_g)
    v = v.reshape(B, S, H, DIFF_V_DIM)
    f32 = jnp.float32
    lam = (jnp.exp(jnp.sum(lq1.astype(f32) * lk1.astype(f32)))
           - jnp.exp(jnp.sum(lq2.astype(f32) * lk2.astype(f32))) + lambda_init)
    slopes = 2.0 ** (-8.0 * jnp.arange(1, H + 1, dtype=f32) / H)
    nb = S // Q_BLOCK
    qb = q.reshape(B, nb, Q_BLOCK, H, 2, d).transpose(1, 0, 2, 3, 4, 5)
    k_pos = jnp.arange(S)
    scale = d ** -0.5

    def block(args):
        qi, i = args
        q_pos = i * Q_BLOCK + jnp.arange(Q_BLOCK)
        s = jnp.einsum('bqhcd,bkhcd->bhcqk', qi, k, preferred_element_type=f32) * scale
        dist = (q_pos[:, None] - k_pos[None, :]).astype(f32)
        s = s - slopes[:, None, None, None] * dist
        s = jnp.where(dist >= 0, s, NEG_INF)
        p = jax.nn.softmax(s, axis=-1)
        a = p[:, :, 0] - lam * p[:, :, 1]
        return jnp.einsum('bhqk,bkhe->bqhe', a.astype(v.dtype), v)

    o = lax.map(block, (qb, jnp.arange(nb)))
    o = o.transpose(1, 0, 2, 3, 4).reshape(B, S, H, DIFF_V_DIM)
    o = _rms(o, subln_g) * (1.0 - lambda_init)
    return o.reshape(B, S, DIFF_WIDTH)


def _conv_ffn(h, w_up, conv_w, conv_b, w_down):
    a = h @ w_up
    C = a.shape[-1]
    a = lax.conv_general_dilated(
        a, conv_w[:, None, :].astype(a.dtype), window_strides=(1,),
        padding=[(CONV_WIDTH - 1, 0)], dimension_numbers=('NWC', 'WIO', 'NWC'),
        feature_group_count=C) + conv_b
    gate, val = jnp.split(a, 2, axis=-1)
    return (jax.nn.silu(gate) * val) @ w_down


def setup_inputs(seed: int = 0) -> dict:
    key = jax.random.key(seed)
    ks = jax.random.split(key, 24)
    L, D, F = DEPTH, D_MODEL, D_FF
    nrm = lambda k, shape, s: jax.random.normal(k, shape, jnp.float32) * s
    return {
        "x": nrm(ks[0], (BATCH, SEQ, D), 1.0),
        "c": nrm(ks[1], (BATCH, D), 1.0),
        "w_ada": nrm(ks[2], (L, D, 6 * D), 0.5 * D ** -0.5),
        "b_ada": nrm(ks[3], (L, 6 * D), 0.02),
        "norm1_g": 1.0 + nrm(ks[4], (L, D), 0.02),
        "norm2_g": 1.0 + nrm(ks[5], (L, D), 0.02),
        "w_in": nrm(ks[6], (L, D, IN_WIDTH), D ** -0.5),
        "sgu_norm_g": 1.0 + nrm(ks[7], (L, SGU_WIDTH), 0.02),
        "w_spatial": nrm(ks[8], (L, SGU_GROUPS, CHUNK, CHUNK), CHUNK ** -0.5),
        "b_spatial": 1.0 + nrm(ks[9], (L, SGU_GROUPS, CHUNK), 0.02),
        "q_norm_g": 1.0 + nrm(ks[10], (L, DIFF_HEAD_DIM), 0.02),
        "k_norm_g": 1.0 + nrm(ks[11], (L, DIFF_HEAD_DIM), 0.02),
        "lambda_q1": nrm(ks[12], (L, DIFF_HEAD_DIM), 0.1),
        "lambda_k1": nrm(ks[13], (L, DIFF_HEAD_DIM), 0.1),
        "lambda_q2": nrm(ks[14], (L, DIFF_HEAD_DIM), 0.1),
        "lambda_k2": nrm(ks[15], (L, DIFF_HEAD_DIM), 0.1),
        "subln_g": 1.0 + nrm(ks[16], (L, DIFF_V_DIM), 0.02),
        "w_out": nrm(ks[17], (L, D, D), D ** -0.5),
        "w_ff_up": nrm(ks[18], (L, D, 2 * F), D ** -0.5),
        "conv_w": nrm(ks[19], (L, CONV_WIDTH, 2 * F), CONV_WIDTH ** -0.5),
        "conv_b": nrm(ks[20], (L, 2 * F), 0.02),
        "w_ff_down": nrm(ks[21], (L, F, D), F ** -0.5),
    }


def reference(x, c, w_ada, b_ada, norm1_g, norm2_g, w_in, sgu_norm_g, w_spatial, b_spatial,
              q_norm_g, k_norm_g, lambda_q1, lambda_k1, lambda_q2, lambda_k2, subln_g,
              w_out, w_ff_up, conv_w, conv_b, w_ff_down):
    split_idx = [int(i) for i in np.cumsum(IN_SPLITS)[:-1]]
    c_act = jax.nn.silu(c)
    for l in range(DEPTH):
        lambda_init = _lambda_init(l)
        mod = c_act @ w_ada[l] + b_ada[l]
        shift1, scale1, gate1, shift2, scale2, gate2 = [m[:, None, :] for m in jnp.split(mod, 6, axis=-1)]
        h = _rms(x, norm1_g[l]) * (1.0 + scale1) + shift1
        z = h @ w_in[l]
        zu, zv, zq, zk, zvb, za, zb = jnp.split(z, split_idx, axis=-1)
        y_a = _spatial_gating(jax.nn.gelu(zu, approximate=False), jax.nn.gelu(zv, approximate=False),
                              sgu_norm_g[l], w_spatial[l], b_spatial[l])
        y_b = _diff_attention(zq, zk, zvb, q_norm_g[l], k_norm_g[l], lambda_q1[l], lambda_k1[l],
                              lambda_q2[l], lambda_k2[l], subln_g[l], lambda_init)
        y = jax.nn.sigmoid(za) * y_a + jax.nn.sigmoid(zb) * y_b
        x = x + gate1 * (y @ w_out[l])
        h2 = _rms(x, norm2_g[l]) * (1.0 + scale2) + shift2
        x = x + gate2 * _conv_ffn(h2, w_ff_up[l], conv_w[l], conv_b[l], w_ff_down[l])
    return x
```

```python
import math
from contextlib import ExitStack
import numpy as np
import concourse.bass as bass
import concourse.mybir as mybir
from concourse.bass_utils import run_bass_kernel_spmd

F32 = mybir.dt.float32
BF16 = mybir.dt.bfloat16
AF = mybir.ActivationFunctionType
ALU = mybir.AluOpType
AX = mybir.AxisListType

EPS = 1e-6
NEG = -30000.0
M0 = 8.0
LAMBDA_INIT = 0.8 - 0.6 * math.exp(-0.3 * 0)


class Cfg:
    def __init__(self, D=4096, H=16, S=2048, F=11008, B=4):
        self.D, self.H, self.S, self.F, self.B = D, H, S, F, B
        self.G = H
        self.FC = D // 128
        self.NB = S // 128
        self.NO = self.NB // 2
        self.NL = self.NO + 1
        self.NT = self.NL * 128
        self.NOT = self.NO * 128
        self.FFC = F // 128
        self.TT = 384
        self.KVH = S // 2
        self.KVT = min(512, self.KVH)
        self.N1T = min(256, S)
        self.FTOK = self.NOT + 2
        self.FT = self.FTOK // 3
        self.DH = self.NOT if self.NOT <= 512 else self.NOT // 2
        assert D == 256 * H and self.NT % self.TT == 0 and self.FTOK % 3 == 0
        assert F % 256 == 0 and self.NOT % self.DH == 0
        o = {}
        n = 0
        for name, w in [("cT", self.FC), ("bada", 6 * self.FC), ("n1g", self.FC), ("n2g", self.FC),
                        ("sgug", self.FC), ("qg", 1), ("kg", 1), ("lam", 4), ("subg", 2),
                        ("convw", 3 * 2 * self.FFC), ("convb", 2 * self.FFC), ("flag", 1), ("eps", 1),
                        ("cb", self.H * self.NL * self.NB), ("tri", 128), ("bsp", self.G * 128),
                        ("wsT", self.G * 128)]:
            o[name] = (n, w)
            n += w
        self.coff = o
        self.NCST = n


class Sem:
    def __init__(self, h, name):
        self.h, self.name, self.count = h, name, 0


class Buf:
    __slots__ = ("name", "last_w", "readers")

    def __init__(self, name):
        self.name, self.last_w, self.readers = name, None, {}


class Op:
    __slots__ = ("eng", "fns", "deps", "sem", "val", "needs_sig", "is_dma", "idx")


class Prog:
    ENG = ("pe", "act", "dve", "pool", "sp")

    def __init__(self, nc, stack):
        self.nc, self.stack = nc, stack
        self.q = {e: [] for e in self.ENG}
        self.nops = 0
        self.dsems = []
        self.fence = None
        self.fence_seen = set()

    def sem(self, name):
        return Sem(self.stack.enter_context(self.nc.semaphore(name)), name)

    def dsem(self, name):
        s = self.sem(name)
        self.dsems.append(s)
        return s

    def op(self, eng, fns, reads=(), writes=(), dsem=None, deps=(), acc=False):
        if callable(fns):
            fns = [fns]
        o = Op()
        o.eng, o.fns, o.is_dma = eng, fns, dsem is not None
        o.sem = o.val = None
        o.needs_sig = False
        o.idx = self.nops
        self.nops += 1
        dd = {}
        for d in deps:
            if d is not None:
                dd[id(d)] = d
        if self.fence is not None and eng not in self.fence_seen:
            self.fence_seen.add(eng)
            dd[id(self.fence)] = self.fence
        for b in reads:
            if b.last_w is not None:
                dd[id(b.last_w)] = b.last_w
        for b in writes:
            if b.last_w is not None and not (acc and b.last_w.eng == eng and not b.last_w.is_dma):
                dd[id(b.last_w)] = b.last_w
            for r in b.readers.values():
                dd[id(r)] = r
        o.deps = list(dd.values())
        for d in o.deps:
            d.needs_sig = True
        key = ("dma", o.idx) if o.is_dma else eng
        for b in reads:
            b.readers[key] = o
        for b in writes:
            b.last_w = o
            b.readers = {}
        if o.is_dma:
            o.sem = dsem
            dsem.count += 16 * len(fns)
            o.val = dsem.count
            o.needs_sig = True
        self.q[eng].append(o)
        return o

    def barrier(self):
        deps = []
        for e in self.ENG:
            for o in reversed(self.q[e]):
                if not o.is_dma:
                    deps.append(o)
                    break
        for s in self.dsems:
            if s.count > 0:
                f = Op()
                f.sem, f.val, f.needs_sig, f.is_dma = s, s.count, True, True
                deps.append(f)
        b = self.op("sp", lambda e: e.nop(), deps=deps)
        b.needs_sig = True
        self.fence = b
        self.fence_seen = {"sp"}
        return b

    def _assign(self):
        self.esem = {}
        for eng in self.ENG:
            s = None
            for o in self.q[eng]:
                if o.is_dma or not o.needs_sig:
                    continue
                if s is None:
                    s = self.sem("e_" + eng)
                s.count += 1
                o.sem, o.val = s, s.count

    def replay(self, eng, e):
        seen = {}
        for o in self.q[eng]:
            for d in o.deps:
                s, v = d.sem, d.val
                if seen.get(s.name, 0) >= v:
                    continue
                seen[s.name] = v
                e.wait_ge(s.h, v)
            last = len(o.fns) - 1
            for i, fn in enumerate(o.fns):
                ins = fn(e)
                if o.is_dma:
                    ins.then_inc(o.sem.h, 16)
                elif o.needs_sig and i == last:
                    ins.then_inc(o.sem.h, 1)

    def run(self, final):
        nc = self.nc
        for o in final:
            o.needs_sig = True
        self._assign()
        with nc.Block() as block:
            @block.tensor
            def _(e):
                self.replay("pe", e)

            @block.scalar
            def _(e):
                self.replay("act", e)

            @block.vector
            def _(e):
                self.replay("dve", e)

            @block.gpsimd
            def _(e):
                self.replay("pool", e)

            @block.sync
            def _(e):
                self.replay("sp", e)
                seen = {}
                for o in final:
                    seen[o.sem.name] = max(seen.get(o.sem.name, (0, None))[0], o.val), o.sem
                for v, s in seen.values():
                    e.wait_ge(s.h, v)


class Arena:
    def __init__(self, ap, nwords):
        self.ap, self.n, self.top = ap, nwords, 0

    def alloc(self, shape, dt):
        n = 1
        for s in shape[1:]:
            n *= s
        words = n if dt == F32 else (n + 1) // 2
        off = self.top
        self.top += words
        assert self.top <= self.n, f"arena overflow {self.top} > {self.n}"
        v = self.ap[:, off:off + words]
        if dt != F32:
            v = v.bitcast(dt)[:, 0:n]
        if len(shape) == 3:
            v = v.rearrange("p (a b) -> p a b", a=shape[1])
        elif len(shape) == 4:
            v = v.rearrange("p (a b c) -> p a b c", a=shape[1], b=shape[2])
        return v


class Ring:
    def __init__(self, P, A, name, n, shape, dt, with_sem=True):
        self.v = [A.alloc(shape, dt) for _ in range(n)]
        self.b = [Buf(f"{name}{i}") for i in range(n)]
        self.s = [P.dsem(f"{name}{i}") for i in range(n)] if with_sem else None
        self.n, self.i = n, 0

    def next(self):
        k = self.i % self.n
        self.i += 1
        return k


def build(cfg):
    c = cfg
    D, H, S, F, FC, NB, NO, NL, NT, FFC = c.D, c.H, c.S, c.F, c.FC, c.NB, c.NO, c.NL, c.NT, c.FFC
    nc = bass.Bass("TRN2", target_bir_lowering=False)
    dt_in = lambda n, s, d=F32: nc.dram_tensor(n, s, d, kind="ExternalInput").ap()
    scr = lambda n, s, d=BF16: nc.dram_tensor(n, s, d, kind="Internal").ap()
    xT_ctx = dt_in("xT_ctx", [D, S])
    xT_loc = dt_in("xT_loc", [D, NT])
    cst_d = dt_in("cst", [128, c.NCST])
    abt_d = dt_in("abt", [H, 4, 384], BF16)
    msk_d = dt_in("msk", [128, 3, 256], BF16)
    w_ada = dt_in("w_ada", [D, 6 * D])
    w_in = dt_in("w_in", [D, 7 * D])
    w_out = dt_in("w_out", [D, D])
    w_up = dt_in("w_up", [D, 2 * F])
    w_down = dt_in("w_down", [F, D])
    outT = nc.dram_tensor("outT", [D, c.NOT], F32, kind="ExternalOutput").ap()
    hT_ctx = scr("hT_ctx", [D, S]); hT_loc = scr("hT_loc", [D, NT])
    kT_s = scr("kT_s", [D, S]); vB_s = scr("vB_s", [S, D]); qT_s = scr("qT_s", [D, NT])
    uT_s = scr("uT_s", [D, NT]); gA_s = scr("gA_s", [D, NT]); gB_s = scr("gB_s", [D, NT])
    vA_s = scr("vA_s", [NT, D]); yT_s = scr("yT_s", [D, NT]); x1T_s = scr("x1T_s", [D, NT], F32)
    actT_s = scr("actT_s", [F, c.NOT])

    fcv = lambda ap: ap.rearrange("(fc p) t -> p fc t", p=128)

    with ExitStack() as st:
        P = Prog(nc, st)
        AW = 52224
        arena_t = st.enter_context(nc.sbuf_tensor("arena", [128, AW], F32))
        A = Arena(arena_t[:, :], AW)
        pb = [st.enter_context(nc.psum_tensor(f"pb{i}", [128, 512], F32)) for i in range(8)]
        Bpb = [Buf(f"pb{i}") for i in range(8)]

        modT = A.alloc([128, 6 * FC], F32); Bmod = Buf("mod")
        s1c = A.alloc([128, FC], F32); s2c = A.alloc([128, FC], F32)
        ones_f = A.alloc([128, 128], F32); ones_b = A.alloc([128, 128], BF16); Bones = Buf("ones")
        scb = A.alloc([128, FC], BF16); Bsc = Buf("sc")
        neglam = A.alloc([128, 1], F32); Blam = Buf("lam")
        subc = A.alloc([128, 2], F32)
        rvA = A.alloc([128, NL], F32); BrvA = Buf("rvA")
        PERSIST_SMALL = A.top
        cst = A.alloc([128, c.NCST], F32)
        Bcst = Buf("cst")
        s_cst = P.dsem("cst")
        P.op("sp", lambda e: e.dma_start(out=cst, in_=cst_d), writes=[Bcst], dsem=s_cst)

        def cc(name, i=0, w=1):
            o, _ = c.coff[name]
            return cst[:, o + i:o + i + w]

        wsTm = A.alloc([128, c.G, 128], F32); Bws = Buf("wsTm")
        P.op("pool", lambda e: e.memset(ones_f, 1.0), writes=[Bones])
        P.op("pool", lambda e: e.memset(ones_b, 1.0), writes=[Bones])
        PERSIST = A.top
        rstd_all = A.alloc([128, S + NT], F32); Brstd = Buf("rstd")
        PERSIST_N = A.top

        shift1 = lambda i: modT[:, 0 * FC + i:0 * FC + i + 1]
        gate1 = lambda i: modT[:, 2 * FC + i:2 * FC + i + 1]
        shift2 = lambda i: modT[:, 3 * FC + i:3 * FC + i + 1]
        gate2 = lambda i: modT[:, 5 * FC + i:5 * FC + i + 1]
        epsc = cc("eps")

        class WStream:
            def __init__(self, ring, specs):
                self.ring, self.specs, self.issued = ring, specs, 0

            def issue(self, upto):
                while self.issued <= min(upto, len(self.specs) - 1):
                    i = self.issued
                    w_ap, KC, c0, ncols = self.specs[i][:4]
                    k = i % self.ring.n
                    dst = self.ring.v[k][:, 0:KC, 0:ncols]
                    src = w_ap.rearrange("(kc p) c -> p kc c", p=128)[:, :, c0:c0 + ncols]
                    P.op("pool", lambda e, dst=dst, src=src: e.dma_start(out=dst, in_=src),
                         writes=[self.ring.b[k]], dsem=self.ring.s[k])
                    self.issued += 1

            def run(self, consume):
                for i, sp in enumerate(self.specs):
                    self.issue(i + self.ring.n - 1)
                    k = i % self.ring.n
                    consume(i, sp, self.ring.v[k], self.ring.b[k])

        pending = []

        def flush_pending(keep=0):
            while len(pending) > keep:
                pending.pop(0)()

        def tiles_of(n, tw):
            return [(t0, min(tw, n - t0)) for t0 in range(0, n, tw)]

        bank_rr = [0]
        qk_aux = [0]

        def gemm_bank(nb):
            k = bank_rr[0] % nb
            bank_rr[0] += 1
            return k

        def mm_group(psap, pairs, reads, bankbuf, skip=False):
            n = len(pairs)
            fns = [(lambda e, l=l, r=r, i=i: e.matmul(psap, lhsT=l, rhs=r, start=(i == 0), stop=(i == n - 1)))
                   for i, (l, r) in enumerate(pairs)]
            return P.op("pe", fns, reads=reads, writes=[bankbuf], acc=True)

        wr0 = Ring(P, A, "w0", 3, [128, FC, 512], BF16)
        xq = Ring(P, A, "xq", 6, [128, 512], F32)
        sqr = Ring(P, A, "sq", 4, [128, 512], F32, with_sem=False)
        r1 = A.alloc([128, 512], F32); Br1 = Buf("r1")
        P.op("act", lambda e: e.activation(out=scb, in_=cc("cT", 0, FC), func=AF.Silu), reads=[Bcst], writes=[Bsc])

        def norm_units():
            jobs = [(xT_ctx, 0, t0, tw) for (t0, tw) in tiles_of(S, 512)] + [(xT_loc, S, t0, tw) for (t0, tw) in tiles_of(NT, 512)]
            for ji, (src, off, t0, tw) in enumerate(jobs):
                bk = 2 + ji % 2
                for fc in range(FC):
                    def unit(src=src, off=off, t0=t0, tw=tw, bk=bk, fc=fc):
                        k = xq.next()
                        P.op("sp", lambda e: e.dma_start(out=xq.v[k][:, 0:tw], in_=src[fc * 128:(fc + 1) * 128, t0:t0 + tw]),
                             writes=[xq.b[k]], dsem=xq.s[k])
                        q = sqr.next()
                        P.op("act", lambda e: e.activation(out=sqr.v[q][:, 0:tw], in_=xq.v[k][:, 0:tw], func=AF.Square),
                             reads=[xq.b[k]], writes=[sqr.b[q]])
                        P.op("pe", lambda e: e.matmul(pb[bk][:, 0:tw], lhsT=ones_f, rhs=sqr.v[q][:, 0:tw], start=(fc == 0), stop=(fc == FC - 1)),
                             reads=[sqr.b[q], Bones], writes=[Bpb[bk]], acc=True)
                        if fc == FC - 1:
                            P.op("act", lambda e: e.activation(out=r1[:, 0:tw], in_=pb[bk][:, 0:tw], func=AF.Sqrt, bias=epsc, scale=1.0 / D),
                                 reads=[Bpb[bk], Bcst], writes=[Br1])
                            P.op("dve", lambda e: e.reciprocal(out=rstd_all[:, off + t0:off + t0 + tw], in_=r1[:, 0:tw]),
                                 reads=[Br1], writes=[Brstd], acc=True)
                    yield unit
        units = list(norm_units())
        specs = [(w_ada, FC, p * 512, 512) for p in range(2 * D // 512)]
        upp = (len(units) + len(specs) - 1) // len(specs)

        def cons0(i, sp, wv, wb):
            for l in range(4):
                j = i * 4 + l
                mm_group(pb[0][:, j:j + 1], [(wv[:, kc, l * 128:(l + 1) * 128], scb[:, kc:kc + 1]) for kc in range(FC)],
                         [wb, Bsc], Bpb[0])
            for _ in range(upp):
                if units:
                    units.pop(0)()
        WStream(wr0, specs).run(cons0)
        while units:
            units.pop(0)()
        P.op("dve", lambda e: e.tensor_tensor(out=modT[:, 0:2 * FC], in0=pb[0][:, 0:2 * FC], in1=cc("bada", 0, 2 * FC), op=ALU.add),
             reads=[Bpb[0], Bcst], writes=[Bmod])
        P.op("dve", lambda e: e.scalar_tensor_tensor(out=s1c, in0=modT[:, FC:2 * FC], scalar=1.0, in1=cc("n1g", 0, FC),
                                                     op0=ALU.add, op1=ALU.mult), reads=[Bmod, Bcst], writes=[Bmod])
        lp = A.alloc([128, 2], F32); le = A.alloc([128, 2], F32); lt = A.alloc([128, 1], F32)
        P.op("dve", lambda e: e.tensor_tensor(out=lp, in0=cc("lam", 0, 2), in1=cc("lam", 2, 2), op=ALU.mult),
             reads=[Bcst], writes=[Blam])
        P.op("pe", lambda e: e.matmul(pb[1][:, 0:2], lhsT=ones_f, rhs=lp, start=True, stop=True),
             reads=[Blam, Bones], writes=[Bpb[1]])
        P.op("act", lambda e: e.activation(out=le, in_=pb[1][:, 0:2], func=AF.Exp), reads=[Bpb[1]], writes=[Blam])
        P.op("dve", lambda e: e.tensor_tensor(out=lt, in0=le[:, 1:2], in1=le[:, 0:1], op=ALU.subtract),
             reads=[Blam], writes=[Blam])
        P.op("dve", lambda e: e.tensor_scalar_add(out=neglam, in0=lt, scalar1=-LAMBDA_INIT), reads=[Blam], writes=[Blam])
        P.op("dve", lambda e: e.tensor_scalar_mul(out=subc, in0=cc("subg", 0, 2), scalar1=1.0 - LAMBDA_INIT),
             reads=[Bcst], writes=[Blam])
        P.op("dve", lambda e: e.tensor_tensor(out=wsTm, in0=cc("wsT", 0, c.G * 128).rearrange("p (g t) -> p g t", g=c.G),
                                              in1=cc("tri", 0, 128)[:, None, :].to_broadcast([128, c.G, 128]), op=ALU.mult),
             reads=[Bcst], writes=[Bws])
        P.barrier()

        A.top = PERSIST_N
        hbuf2 = A.alloc([128, FC, NT], BF16); Bhb2 = [Buf("hbuf2a"), Buf("hbuf2b"), Buf("hbuf2c")]; s_hb = P.dsem("hb")
        wr = Ring(P, A, "w2", 3, [128, FC, 256], BF16)
        stg = Ring(P, A, "stg", 4, [128, 512], BF16)
        sqb = Ring(P, A, "sqb", 2, [128, 512], BF16, with_sem=False)
        rq = Ring(P, A, "rq", 2, [128, 512], F32, with_sem=False)
        rq2 = Ring(P, A, "rq2", 2, [128, 512], F32, with_sem=False)
        junk = A.alloc([128, 256], BF16); Bjunk = Buf("junk")
        NPV = D // 256
        ssq = A.alloc([128, NL, NPV], F32); Bssq = Buf("ssq")
        P.op("pool", lambda e: e.memset(ssq, 0.0), writes=[Bssq])

        def load_hbuf(hb, Bh, src, t0, ntok):
            g = max(1, FC // 4)
            fns = [(lambda e, a=a: e.dma_start(out=hb[:, a:a + g, 0:ntok], in_=fcv(src)[:, a:a + g, t0:t0 + ntok]))
                   for a in range(0, FC, g)]
            P.op("sp", fns, writes=[Bh], dsem=s_hb)

        xin2 = Ring(P, A, "xin2", 4, [128, 512], F32)
        tm2 = Ring(P, A, "tm2", 3, [128, 512], F32, with_sem=False)

        def fill_hbuf(hb, Bh, src, t0, ntok, roff, tilew):
            for ti, (a, tw) in enumerate(tiles_of(ntok, tilew)):
                for fc in range(FC):
                    def one(fc=fc, a=a, tw=tw, ti=ti):
                        k = xin2.next()
                        P.op("sp", lambda e: e.dma_start(out=xin2.v[k][:, 0:tw], in_=src[fc * 128:(fc + 1) * 128, t0 + a:t0 + a + tw]),
                             writes=[xin2.b[k]], dsem=xin2.s[k])
                        q = tm2.next()
                        P.op("dve", lambda e: e.scalar_tensor_tensor(out=tm2.v[q][:, 0:tw], in0=xin2.v[k][:, 0:tw], scalar=s1c[:, fc:fc + 1],
                                                                     in1=rstd_all[:, roff + a:roff + a + tw], op0=ALU.mult, op1=ALU.mult),
                             reads=[xin2.b[k], Brstd, Bmod], writes=[tm2.b[q]])
                        P.op("act", lambda e: e.activation(out=hb[:, fc, a:a + tw], in_=tm2.v[q][:, 0:tw], func=AF.Identity, bias=shift1(fc), scale=1.0),
                             reads=[tm2.b[q], Bmod], writes=[Bh[ti]], acc=True)
                    one()

        ada_left = [(w_ada, FC, 2 * D + p * 256, 256, ("ada", p)) for p in range(3 * D // 256)]
        ada_ctr = [0]

        def cons_ada(sp, wv, wb):
            p = sp[4][1]
            for l in range(2):
                j = p * 2 + l
                mm_group(pb[6][:, j:j + 1], [(wv[:, kc, l * 128:(l + 1) * 128], scb[:, kc:kc + 1]) for kc in range(FC)], [wb, Bsc], Bpb[6])

        def with_ada(specs, cons):
            out = []
            for sp in specs:
                out.append(sp)
                ada_ctr[0] += 1
                if ada_left and ada_ctr[0] % 3 == 0:
                    out.append(ada_left.pop(0))

            def f(i, sp, wv, wb):
                if isinstance(sp[4], tuple) and sp[4][0] == "ada":
                    cons_ada(sp, wv, wb)
                else:
                    cons(i, sp, wv, wb)
            return out, f

        def run_ada(specs, cons):
            sp2, c2 = with_ada(specs, cons)
            WStream(wr, sp2).run(c2)

        def epi_simple(func, dst):
            def f(sp, l, ccg, t0, tw, ps, bb, tg0):
                k = stg.next()
                P.op("act", lambda e: e.activation(out=stg.v[k][:, 0:tw], in_=ps[:, 0:tw], func=func), reads=[bb], writes=[stg.b[k]])
                P.op("sp", lambda e: e.dma_start(out=dst[ccg * 128:(ccg + 1) * 128, tg0 + t0:tg0 + t0 + tw], in_=stg.v[k][:, 0:tw]),
                     reads=[stg.b[k]], dsem=stg.s[k])
            return f

        def epi_qk(gname, dst):
            def f(sp, l, ccg, t0, tw, ps, bb, tg0):
                a = sqb.next()
                P.op("act", lambda e: e.activation(out=sqb.v[a][:, 0:tw], in_=ps[:, 0:tw], func=AF.Square), reads=[bb], writes=[sqb.b[a]])

                def rest():
                    xb = 4 + qk_aux[0] % 2
                    qk_aux[0] += 1
                    P.op("pe", lambda e: e.matmul(pb[xb][:, 0:tw], lhsT=ones_b, rhs=sqb.v[a][:, 0:tw], start=True, stop=True),
                         reads=[sqb.b[a], Bones], writes=[Bpb[xb]])
                    r = rq.next()
                    P.op("act", lambda e: e.activation(out=rq.v[r][:, 0:tw], in_=pb[xb][:, 0:tw], func=AF.Sqrt, bias=epsc, scale=1.0 / 128),
                         reads=[Bpb[xb], Bcst], writes=[rq.b[r]])
                    r2 = rq2.next()
                    P.op("dve", lambda e: e.reciprocal(out=rq2.v[r2][:, 0:tw], in_=rq.v[r][:, 0:tw]), reads=[rq.b[r]], writes=[rq2.b[r2]])
                    k = stg.next()
                    P.op("dve", lambda e: e.scalar_tensor_tensor(out=stg.v[k][:, 0:tw], in0=ps[:, 0:tw], scalar=cc(gname), in1=rq2.v[r2][:, 0:tw],
                                                                 op0=ALU.mult, op1=ALU.mult), reads=[bb, rq2.b[r2], Bcst], writes=[stg.b[k]])
                    P.op("sp", lambda e: e.dma_start(out=dst[ccg * 128:(ccg + 1) * 128, tg0 + t0:tg0 + t0 + tw], in_=stg.v[k][:, 0:tw]),
                         reads=[stg.b[k]], dsem=stg.s[k])
                pending.append(rest)
            return f

        def cons_F(hb, Bh, tiles, epi, tg0, nbank=4):
            def f(i, sp, wv, wb):
                w_ap, KC, c0, ncols, ccg0 = sp
                for l in range(ncols // 128):
                    for ti, (t0, tw) in enumerate(tiles):
                        bk = gemm_bank(nbank)
                        mm_group(pb[bk][:, 0:tw], [(wv[:, kc, l * 128:(l + 1) * 128], hb[:, kc, t0:t0 + tw]) for kc in range(KC)],
                                 ([wb] + list(Bh)) if isinstance(Bh, tuple) else [wb, Bh[ti] if isinstance(Bh, list) else Bh], Bpb[bk])
                        flush_pending()
                        epi(sp, l, ccg0 + l, t0, tw, pb[bk], Bpb[bk], tg0)
            return f

        def cons_T(hb, Bh, nblk, kind, tg0, tilew):
            def f(i, sp, wv, wb):
                w_ap, KC, c0, ncols, pidx = sp
                for tb in range(nblk):
                    bk = gemm_bank(4)
                    mm_group(pb[bk][:, 0:ncols], [(hb[:, kc, tb * 128:(tb + 1) * 128], wv[:, kc, 0:ncols]) for kc in range(KC)],
                             [wb, Bh[(tb * 128) // tilew]], Bpb[bk])
                    k = stg.next()
                    sv = stg.v[k][:, 0:ncols]
                    if kind == "vB":
                        P.op("dve", lambda e, sv=sv, bk=bk: e.tensor_copy(out=sv, in_=pb[bk][:, 0:ncols]), reads=[Bpb[bk]], writes=[stg.b[k]])
                        dst = vB_s[tg0 + tb * 128:tg0 + (tb + 1) * 128, pidx * 256:pidx * 256 + ncols]
                    else:
                        P.op("act", lambda e, sv=sv, bk=bk: e.activation(out=sv, in_=pb[bk][:, 0:ncols], func=AF.Gelu), reads=[Bpb[bk]], writes=[stg.b[k]])
                        P.op("act", lambda e, sv=sv, tb=tb, pidx=pidx: e.activation(out=junk[:, 0:ncols], in_=sv, func=AF.Square,
                                                                                  accum_out=ssq[:, tb, pidx:pidx + 1]),
                             reads=[stg.b[k]], writes=[Bjunk, Bssq])
                        dst = vA_s[tb * 128:(tb + 1) * 128, pidx * 256:pidx * 256 + ncols]
                    P.op("sp", lambda e, sv=sv, dst=dst: e.dma_start(out=dst, in_=sv), reads=[stg.b[k]], dsem=stg.s[k])
            return f

        OU, OVA, OQ, OK_, OVB, OGA, OGB = [i * D for i in range(7)]
        for half in range(2):
            fill_hbuf(hbuf2, Bhb2, xT_ctx, half * c.KVH, c.KVH, half * c.KVH, c.KVT)
            specs = [(w_in, FC, OK_ + p * 256, 256, p * 2) for p in range(NPV)]
            run_ada(specs, cons_F(hbuf2, Bhb2, tiles_of(c.KVH, c.KVT), epi_qk("kg", kT_s), half * c.KVH))
            flush_pending()
            specs = [(w_in, FC, OVB + p * 256, 256, p) for p in range(NPV)]
            run_ada(specs, cons_T(hbuf2, Bhb2, c.KVH // 128, "vB", half * c.KVH, c.KVT))
        fill_hbuf(hbuf2, Bhb2, xT_loc, 0, NT, S, c.TT)
        mt = [(126 + i * c.FT, c.FT) for i in range(3)]
        zt = A.alloc([128, 128], BF16); Bzt = Buf("zt"); s_zt = P.dsem("zt")
        P.op("pool", lambda e: e.memset(zt, 0.0), writes=[Bzt])
        P.op("sp", [(lambda e, d=d: e.dma_start(out=fcv(d)[:, :, 0:126], in_=zt[:, None, 0:126].to_broadcast([128, FC, 126])))
                    for d in (uT_s, gA_s, gB_s, qT_s)], reads=[Bzt], dsem=s_zt)
        run_ada([(w_in, FC, OVA + p * 256, 256, p) for p in range(NPV)], cons_T(hbuf2, Bhb2, NL, "vA", 0, c.TT))
        run_ada([(w_in, FC, OU + p * 256, 256, p * 2) for p in range(NPV)], cons_F(hbuf2, tuple(Bhb2), mt, epi_simple(AF.Gelu, uT_s), 0))
        run_ada([(w_in, FC, OGA + p * 256, 256, p * 2) for p in range(NPV)], cons_F(hbuf2, tuple(Bhb2), mt, epi_simple(AF.Sigmoid, gA_s), 0))
        run_ada([(w_in, FC, OGB + p * 256, 256, p * 2) for p in range(NPV)], cons_F(hbuf2, tuple(Bhb2), mt, epi_simple(AF.Sigmoid, gB_s), 0))
        run_ada([(w_in, FC, OQ + p * 256, 256, p * 2) for p in range(NPV)], cons_F(hbuf2, tuple(Bhb2), mt, epi_qk("qg", qT_s), 0))
        flush_pending()
        if ada_left:
            rest = list(ada_left)
            del ada_left[:]
            WStream(wr, rest).run(lambda i, sp, wv, wb: cons_ada(sp, wv, wb))
        P.op("dve", lambda e: e.tensor_tensor(out=modT[:, 2 * FC:5 * FC], in0=pb[6][:, 0:3 * FC], in1=cc("bada", 2 * FC, 3 * FC), op=ALU.add),
             reads=[Bpb[6], Bcst], writes=[Bmod])
        P.op("dve", lambda e: e.scalar_tensor_tensor(out=s2c, in0=modT[:, 4 * FC:5 * FC], scalar=1.0, in1=cc("n2g", 0, FC),
                                                     op0=ALU.add, op1=ALU.mult), reads=[Bmod, Bcst], writes=[Bmod])
        sst = A.alloc([128, NL], F32); sst2 = A.alloc([128, NL], F32)
        P.op("dve", lambda e: e.tensor_reduce(out=sst, in_=ssq, axis=AX.X, op=ALU.add), reads=[Bssq], writes=[BrvA])
        P.op("act", lambda e: e.activation(out=sst2, in_=sst, func=AF.Sqrt, bias=epsc, scale=1.0 / D), reads=[BrvA, Bcst], writes=[BrvA])
        P.op("dve", lambda e: e.reciprocal(out=rvA, in_=sst2), reads=[BrvA], writes=[BrvA])
        P.barrier()

        A.top = PERSIST
        hs_k = [A.alloc([128, 2, S], BF16) for _ in range(2)]
        hs_q = [A.alloc([128, 2, NT], BF16) for _ in range(2)]
        hs_v = [A.alloc([128, NB, 256], BF16) for _ in range(2)]
        hs_va = [A.alloc([128, NL, 256], BF16) for _ in range(2)]
        hs_u = [A.alloc([128, 2, NT], BF16) for _ in range(2)]
        hs_ga = [A.alloc([128, 2, NT], BF16) for _ in range(2)]
        hs_gb = [A.alloc([128, 2, NT], BF16) for _ in range(2)]
        hs_ab = [A.alloc([128, 384], BF16) for _ in range(2)]
        mskt = A.alloc([128, 3, 256], BF16); Bmsk = Buf("msk"); s_msk = P.dsem("msk")
        P.op("sp", lambda e: e.dma_start(out=mskt, in_=msk_d), writes=[Bmsk], dsem=s_msk)
        Bhs = [Buf("hs0"), Buf("hs1")]; s_hs = [P.dsem("hs0"), P.dsem("hs1")]
        yth = Ring(P, A, "yth", 2, [128, 2, NT], BF16)
        yAg2 = [A.alloc([128, 2, NT], BF16) for _ in range(2)]; ByA2 = [Buf("yAg0"), Buf("yAg1")]
        oh2 = [A.alloc([128, 2, NT], F32) for _ in range(2)]; Boh2 = [Buf("oh0"), Buf("oh1")]
        sqh = A.alloc([128, 2, NT], BF16); Bsqh = Buf("sqh")
        rsub = A.alloc([128, NT], F32); rsub2 = A.alloc([128, NT], F32); Brsub = Buf("rsub")
        wsj = Ring(P, A, "wsj", 3, [128, 128], BF16, with_sem=False)
        mix = Ring(P, A, "mix", 2, [128, 512], F32, with_sem=False)
        pTr = Ring(P, A, "pT", 6, [128, 256], BF16, with_sem=False)
        S4 = [0, 1, 6, 7]
        rl = A.alloc([128, 256], F32); Brl = Buf("rl")
        onn = A.alloc([128, 2, 2, 128], F32); Bon = Buf("on")
        hrow = lambda ap, h: ap.rearrange("(hc p) t -> p hc t", p=128)[:, 2 * h:2 * h + 2, :]
        SCALE = 128.0 ** -0.5

        def load_head(h):
            k = h % 2
            fns = [
                lambda e: e.dma_start(out=hs_k[k], in_=hrow(kT_s, h)),
                lambda e: e.dma_start(out=hs_q[k], in_=hrow(qT_s, h)),
                lambda e: e.dma_start(out=hs_v[k], in_=vB_s.rearrange("(kb p) c -> p kb c", p=128)[:, :, h * 256:(h + 1) * 256]),
                lambda e: e.dma_start(out=hs_va[k], in_=vA_s.rearrange("(kb p) c -> p kb c", p=128)[:, :, h * 256:(h + 1) * 256]),
                lambda e: e.dma_start(out=hs_u[k], in_=hrow(uT_s, h)),
                lambda e: e.dma_start(out=hs_ga[k], in_=hrow(gA_s, h)),
                lambda e: e.dma_start(out=hs_gb[k], in_=hrow(gB_s, h)),
                lambda e: e.dma_start(out=hs_ab[k][0:4, :], in_=abt_d[h]),
            ]
            P.op("sp", fns, writes=[Bhs[k]], dsem=s_hs[k])

        aux_i = [0]
        sb_i = [0]
        acc_i = [0]
        load_head(0)

        def sgu_group(h, k, j0):
            js = list(range(j0, min(NL, j0 + 4)))
            nb_ = len(js)
            sl = slice(j0 * 128, (j0 + nb_) * 128)
            wks = []
            for j in js:
                wk = wsj.next()
                P.op("dve", lambda e, wk=wk, j=j: e.tensor_scalar_mul(out=wsj.v[wk], in0=wsTm[:, h, :], scalar1=rvA[:, j:j + 1]),
                     reads=[Bws, BrvA], writes=[wsj.b[wk]])
                wks.append(wk)

            def chunk(e2):
                xb = S4[sb_i[0] % 4]
                sb_i[0] += 1
                fns = [(lambda e, j=j, wk=wk: e.matmul(pb[xb][:, (j - j0) * 128:(j - j0 + 1) * 128],
                                                       lhsT=hs_va[k][:, j, e2 * 128:(e2 + 1) * 128], rhs=wsj.v[wk],
                                                       start=True, stop=True)) for j, wk in zip(js, wks)]
                P.op("pe", fns, reads=[Bhs[k]] + [wsj.b[w] for w in wks], writes=[Bpb[xb]])
                m = mix.next()
                mv = mix.v[m][:, 0:nb_ * 128]
                P.op("dve", lambda e: e.scalar_tensor_tensor(
                    out=mv.rearrange("p (a b) -> p a b", a=nb_),
                    in0=pb[xb][:, 0:nb_ * 128].rearrange("p (a b) -> p a b", a=nb_),
                    scalar=cc("sgug", 2 * h + e2), in1=cc("bsp", h * 128, 128)[:, None, :].to_broadcast([128, nb_, 128]),
                    op0=ALU.mult, op1=ALU.add), reads=[Bpb[xb], Bcst], writes=[mix.b[m]])
                P.op("pool", lambda e: e.tensor_tensor(out=mv, in0=mv, in1=hs_u[k][:, e2, sl], op=ALU.mult),
                     reads=[Bhs[k]], writes=[mix.b[m]])
                P.op("pool", lambda e: e.tensor_tensor(out=yAg2[k][:, e2, sl], in0=mv, in1=hs_ga[k][:, e2, sl], op=ALU.mult),
                     reads=[Bhs[k], mix.b[m]], writes=[ByA2[k]], acc=True)
            for e2 in range(2):
                chunk(e2)

        def attn_step(h, k, j, kb, nkb, ob, lb):
            kbB = (j - 1) if j >= 1 else 0
            rsel = 1 if kb == NO - 1 + j else (2 if kb == kbB else 0)
            sbk = S4[sb_i[0] % 4]
            sb_i[0] += 1
            fns = [(lambda e, cm=cm: e.matmul(pb[sbk][:, cm * 128:(cm + 1) * 128],
                                              lhsT=hs_k[k][:, cm, kb * 128:(kb + 1) * 128],
                                              rhs=hs_q[k][:, cm, j * 128:(j + 1) * 128], start=(cm == 0), stop=False, skip_group_check=True))
                   for cm in range(2)]
            fns.append(lambda e: e.matmul(pb[sbk][:, 0:256], lhsT=hs_ab[k][0:4, 0:128], rhs=hs_ab[k][0:4, 128:384],
                                          start=False, stop=(rsel == 0), skip_group_check=True))
            if rsel:
                fns.append(lambda e: e.matmul(pb[sbk][:, 0:256], lhsT=mskt[:, 0, 0:128], rhs=mskt[:, rsel, :],
                                              start=False, stop=True, skip_group_check=True))
            P.op("pe", fns, reads=[Bhs[k], Bmsk], writes=[Bpb[sbk]])
            flush_pending(keep=2)
            pt = pTr.next()
            cbi = (h * NL + j) * NB + kb
            P.op("act", lambda e: e.activation(out=pTr.v[pt], in_=pb[sbk][:, 0:256], func=AF.Exp, bias=cc("cb", cbi), scale=SCALE),
                 reads=[Bpb[sbk], Bcst], writes=[pTr.b[pt]])

            def pv():
                first, last = kb == 0, kb == nkb - 1
                fns = [
                    lambda e: e.matmul(pb[ob][:, 0:256], lhsT=hs_v[k][:, kb, 0:128], rhs=pTr.v[pt], start=first, stop=last, skip_group_check=True),
                    lambda e: e.matmul(pb[ob][:, 256:512], lhsT=hs_v[k][:, kb, 128:256], rhs=pTr.v[pt], start=False, stop=last, skip_group_check=True),
                    lambda e: e.matmul(pb[lb][:, 0:256], lhsT=ones_b, rhs=pTr.v[pt], start=first, stop=last),
                ]
                P.op("pe", fns, reads=[Bhs[k], pTr.b[pt], Bones], writes=[Bpb[ob], Bpb[lb]], acc=True)
            pending.append(pv)

        def attn_block(h, k, j):
            ob = 2 + acc_i[0] % 2
            lb = 4 + acc_i[0] % 2
            acc_i[0] += 1
            nkb = NO + j
            for kb in range(nkb):
                attn_step(h, k, j, kb, nkb, ob, lb)
            flush_pending()
            P.op("dve", lambda e: e.reciprocal(out=rl, in_=pb[lb][:, 0:256]), reads=[Bpb[lb]], writes=[Brl])
            P.op("dve", lambda e: e.tensor_tensor(out=onn.rearrange("p a b c -> p a (b c)"),
                                                  in0=pb[ob][:, 0:512].rearrange("p (a b) -> p a b", a=2),
                                                  in1=rl[:, None, :].to_broadcast([128, 2, 256]), op=ALU.mult),
                 reads=[Bpb[ob], Brl], writes=[Bon])
            P.op("dve", lambda e: e.scalar_tensor_tensor(out=oh2[k][:, :, j * 128:(j + 1) * 128], in0=onn[:, :, 1, :], scalar=neglam,
                                                         in1=onn[:, :, 0, :], op0=ALU.mult, op1=ALU.add),
                 reads=[Bon, Blam], writes=[Boh2[k]], acc=True)

        def subln_tile(t0, tw):
            xb = S4[sb_i[0] % 4]
            sb_i[0] += 1
            fns = [(lambda e, e2=e2: e.matmul(pb[xb][:, 0:tw], lhsT=ones_b, rhs=sqh[:, e2, t0:t0 + tw],
                                              start=(e2 == 0), stop=(e2 == 1))) for e2 in range(2)]
            P.op("pe", fns, reads=[Bsqh, Bones], writes=[Bpb[xb]])
            P.op("act", lambda e: e.activation(out=rsub[:, t0:t0 + tw], in_=pb[xb][:, 0:tw], func=AF.Sqrt, bias=epsc, scale=1.0 / 256),
                 reads=[Bpb[xb], Bcst], writes=[Brsub])

        def tail(h):
            k = h % 2
            oh = oh2[k]
            P.op("act", lambda e: e.activation(out=sqh, in_=oh, func=AF.Square), reads=[Boh2[k]], writes=[Bsqh])
            for (t0, tw) in tiles_of(NT, 512):
                subln_tile(t0, tw)
            P.op("dve", lambda e: e.reciprocal(out=rsub2, in_=rsub), reads=[Brsub], writes=[Brsub])
            yk = yth.next()
            for e2 in range(2):
                P.op("dve", lambda e, e2=e2: e.scalar_tensor_tensor(out=oh[:, e2, :], in0=oh[:, e2, :], scalar=subc[:, e2:e2 + 1], in1=rsub2,
                                                                  op0=ALU.mult, op1=ALU.mult), reads=[Brsub, Blam], writes=[Boh2[k]])
            P.op("pool", lambda e: e.tensor_tensor(out=oh, in0=oh, in1=hs_gb[k], op=ALU.mult), reads=[Bhs[k]], writes=[Boh2[k]])
            P.op("pool", lambda e: e.tensor_tensor(out=yth.v[yk], in0=oh, in1=yAg2[k], op=ALU.add), reads=[Boh2[k], ByA2[k]], writes=[yth.b[yk]])
            P.op("sp", lambda e: e.dma_start(out=hrow(yT_s, h), in_=yth.v[yk]), reads=[yth.b[yk]], dsem=yth.s[yk])

        def sgu(h):
            for j0 in range(0, NL, 4):
                sgu_group(h, h % 2, j0)

        sgu(0)
        for h in range(H):
            k = h % 2
            for j in range(NL):
                attn_block(h, k, j)
                if j == 0:
                    if h > 0:
                        tail(h - 1)
                    if h + 1 < H:
                        load_head(h + 1)
                if j == NL - 2 and h + 1 < H:
                    sgu(h + 1)
        tail(H - 1)
        P.barrier()

        A.top = PERSIST
        hbuf4 = A.alloc([128, FC, NT], BF16); Bhb4 = Buf("hbuf4")
        wr = Ring(P, A, "w4", 3, [128, FC, 256], BF16)
        xin4 = Ring(P, A, "xin", 3, [128, 512], F32)
        x1s = Ring(P, A, "x1s", 3, [128, 512], F32)
        sq4 = Ring(P, A, "sq4", 2, [128, 512], F32, with_sem=False)
        mtl = [(126 + i * c.FT, c.FT) for i in range(3)]
        Bhb4t = [Buf("hbuf4a"), Buf("hbuf4b"), Buf("hbuf4c")]
        s_hb4 = [P.dsem("hb4a"), P.dsem("hb4b"), P.dsem("hb4c")]
        for ti4, (t04, tw4) in enumerate(mtl):
            P.op("sp", [(lambda e, a=a, t04=t04, tw4=tw4: e.dma_start(out=hbuf4[:, a:a + FC // 2, t04:t04 + tw4],
                                                                   in_=fcv(yT_s)[:, a:a + FC // 2, t04:t04 + tw4]))
                        for a in range(0, FC, FC // 2)], writes=[Bhb4t[ti4]], dsem=s_hb4[ti4])
        assert len(mtl) <= 3

        def epi_out(sp, l, ccg, t0, tw, ps, bb, tg0):
            ti = (t0 - 126) // c.FT
            xi = xin4.next()
            P.op("sp", lambda e: e.dma_start(out=xin4.v[xi][:, 0:tw], in_=xT_loc[ccg * 128:(ccg + 1) * 128, t0:t0 + tw]),
                 writes=[xin4.b[xi]], dsem=xin4.s[xi])
            xs = x1s.next()
            P.op("dve", lambda e: e.scalar_tensor_tensor(out=x1s.v[xs][:, 0:tw], in0=ps[:, 0:tw], scalar=gate1(ccg), in1=xin4.v[xi][:, 0:tw],
                                                         op0=ALU.mult, op1=ALU.add), reads=[bb, xin4.b[xi], Bmod], writes=[x1s.b[xs]])
            q = sq4.next()
            P.op("act", lambda e: e.activation(out=sq4.v[q][:, 0:tw], in_=x1s.v[xs][:, 0:tw], func=AF.Square), reads=[x1s.b[xs]], writes=[sq4.b[q]])
            P.op("sp", lambda e: e.dma_start(out=x1T_s[ccg * 128:(ccg + 1) * 128, t0:t0 + tw], in_=x1s.v[xs][:, 0:tw]),
                 reads=[x1s.b[xs]], dsem=x1s.s[xs])

            def rest():
                P.op("pe", lambda e: e.matmul(pb[4 + ti][:, 0:tw], lhsT=ones_f, rhs=sq4.v[q][:, 0:tw], start=(ccg == 0), stop=(ccg == FC - 1)),
                     reads=[sq4.b[q], Bones], writes=[Bpb[4 + ti]], acc=True)
            pending.append(rest)
        WStream(wr, [(w_out, FC, p * 256, 256, p * 2) for p in range(D // 256)]).run(cons_F(hbuf4, Bhb4t, mtl, epi_out, 0))
        flush_pending()
        P.barrier()
        r2a = A.alloc([128, NT], F32); rs2 = A.alloc([128, NT], F32); Br2 = Buf("r2")
        tm4 = Ring(P, A, "tm4", 2, [128, 512], F32, with_sem=False)
        for ti, (t0, tw) in enumerate(mtl):
            P.op("act", lambda e, ti=ti, t0=t0, tw=tw: e.activation(out=r2a[:, t0:t0 + tw], in_=pb[4 + ti][:, 0:tw], func=AF.Sqrt, bias=epsc, scale=1.0 / D),
                 reads=[Bpb[4 + ti], Bcst], writes=[Br2])
        P.op("dve", lambda e: e.reciprocal(out=rs2, in_=r2a), reads=[Br2], writes=[Br2])
        for fc in range(FC):
            for (t0, tw) in mtl:
                xi = xin4.next()
                P.op("sp", lambda e, xi=xi, fc=fc, t0=t0, tw=tw: e.dma_start(out=xin4.v[xi][:, 0:tw], in_=x1T_s[fc * 128:(fc + 1) * 128, t0:t0 + tw]),
                     writes=[xin4.b[xi]], dsem=xin4.s[xi])
                q = tm4.next()
                P.op("dve", lambda e, xi=xi, q=q, fc=fc, t0=t0, tw=tw: e.scalar_tensor_tensor(
                    out=tm4.v[q][:, 0:tw], in0=xin4.v[xi][:, 0:tw], scalar=s2c[:, fc:fc + 1], in1=rs2[:, t0:t0 + tw], op0=ALU.mult, op1=ALU.mult),
                    reads=[xin4.b[xi], Br2, Bmod], writes=[tm4.b[q]])
                P.op("act", lambda e, q=q, fc=fc, t0=t0, tw=tw: e.activation(out=hbuf4[:, fc, t0:t0 + tw], in_=tm4.v[q][:, 0:tw], func=AF.Identity,
                                                                           bias=shift2(fc), scale=1.0), reads=[tm4.b[q], Bmod], writes=[Bhb4] + Bhb4t, acc=True)
        P.op("dve", lambda e: e.tensor_scalar_mul(out=hbuf4[:, :, 126:128], in0=hbuf4[:, :, 126:128], scalar1=cc("flag")), reads=[Bcst], writes=[Bhb4] + Bhb4t)
        P.barrier()

        HB_TOP = A.top = PERSIST + (FC * NT + 1) // 2
        wr = Ring(P, A, "w5", 3, [128, FC, 256], BF16)
        FTOK, FT, NOT = c.FTOK, c.FT, c.NOT
        ag = [A.alloc([128, FTOK], F32) for _ in range(2)]; av = [A.alloc([128, FTOK], F32) for _ in range(2)]
        Bag = [Buf("ag0"), Buf("ag1")]; Bav = [Buf("av0"), Buf("av1")]
        cgt = A.alloc([128, NOT], F32); cvt = A.alloc([128, NOT], F32); sgt = A.alloc([128, NOT], F32)
        Bcg, Bcv, Bsg = Buf("cg"), Buf("cv"), Buf("sg")
        ast = Ring(P, A, "ast", 2, [128, NOT], BF16)
        ftiles = [(126 + i * FT, FT) for i in range(3)]
        cw = lambda jj, ch: cc("convw", jj * 2 * FFC + ch)
        cbv = lambda ch: cc("convb", ch)

        def conv(src, dst, Bs, Bd, ch):
            P.op("dve", lambda e: e.tensor_scalar(out=dst, in0=src[:, 0:NOT], scalar1=cw(0, ch), scalar2=cbv(ch), op0=ALU.mult, op1=ALU.add),
                 reads=[Bs, Bcst], writes=[Bd])
            P.op("dve", lambda e: e.scalar_tensor_tensor(out=dst, in0=src[:, 1:NOT + 1], scalar=cw(1, ch), in1=dst, op0=ALU.mult, op1=ALU.add),
                 reads=[Bs, Bcst], writes=[Bd])
            P.op("dve", lambda e: e.scalar_tensor_tensor(out=dst, in0=src[:, 2:NOT + 2], scalar=cw(2, ch), in1=dst, op0=ALU.mult, op1=ALU.add),
                 reads=[Bs, Bcst], writes=[Bd])

        def cons_up(i, sp, wv, wb):
            w_ap, KC, c0, ncols, (isval, c2) = sp
            for l in range(2):
                tgt, Bt = (av[l], Bav[l]) if isval else (ag[l], Bag[l])
                for ti, (t0, tw) in enumerate(ftiles):
                    bk = gemm_bank(6)
                    mm_group(pb[bk][:, 0:tw], [(wv[:, kc, l * 128:(l + 1) * 128], hbuf4[:, kc, t0:t0 + tw]) for kc in range(KC)],
                             [wb, Bhb4], Bpb[bk])
                    P.op("act", lambda e, bk=bk, tgt=tgt, ti=ti, tw=tw: e.activation(out=tgt[:, ti * FT:ti * FT + tw], in_=pb[bk][:, 0:tw], func=AF.Copy),
                         reads=[Bpb[bk]], writes=[Bt])
                if isval:
                    ch = c2 * 2 + l
                    conv(ag[l], cgt, Bag[l], Bcg, ch)
                    conv(av[l], cvt, Bav[l], Bcv, FFC + ch)
                    P.op("act", lambda e: e.activation(out=sgt, in_=cgt, func=AF.Silu), reads=[Bcg], writes=[Bsg])
                    a = ast.next()
                    P.op("dve", lambda e, a=a: e.tensor_tensor(out=ast.v[a], in0=sgt, in1=cvt, op=ALU.mult), reads=[Bsg, Bcv], writes=[ast.b[a]])
                    P.op("sp", lambda e, a=a, ch=ch: e.dma_start(out=actT_s[ch * 128:(ch + 1) * 128, :], in_=ast.v[a]), reads=[ast.b[a]], dsem=ast.s[a])
        specs = []
        g2 = [(w_ada, FC, 5 * D + p * 256, 256, ("ada", p)) for p in range(D // 256)]
        every5 = max(1, (2 * (F // 256)) // (len(g2) + 1))
        for c2 in range(F // 256):
            specs.append((w_up, FC, c2 * 256, 256, (False, c2)))
            specs.append((w_up, FC, F + c2 * 256, 256, (True, c2)))
            if g2 and (len(specs) // 2) % max(1, every5 // 2) == 0:
                specs.append(g2.pop(0))
        specs += g2

        def cons5(i, sp, wv, wb):
            if sp[4][0] == "ada":
                p = sp[4][1]
                for l in range(2):
                    j = p * 2 + l
                    mm_group(pb[7][:, j:j + 1], [(wv[:, kc, l * 128:(l + 1) * 128], scb[:, kc:kc + 1]) for kc in range(FC)], [wb, Bsc], Bpb[7])
            else:
                cons_up(i, sp, wv, wb)
        WStream(wr, specs).run(cons5)
        P.op("dve", lambda e: e.tensor_tensor(out=modT[:, 5 * FC:6 * FC], in0=pb[7][:, 0:FC], in1=cc("bada", 5 * FC, FC), op=ALU.add),
             reads=[Bpb[7], Bcst], writes=[Bmod])
        P.barrier()

        A.top = PERSIST_SMALL
        DH = c.DH
        NTT = c.NOT // DH
        KG = 2
        NCC = min(8 // NTT, FC)
        CGW = NCC * 128
        hbuf6 = A.alloc([128, FFC, c.NOT], BF16); Bhb6 = Buf("hbuf6"); s_hb6 = P.dsem("hb6")
        wr6 = Ring(P, A, "w6", 4, [128, KG, CGW], BF16)
        xin6 = Ring(P, A, "xin6", 8, [128, DH], F32)
        ost = Ring(P, A, "ost", 3, [128, DH], F32)
        outs = []

        xpre = {}

        def dn_pre(ccg, hf):
            xi = xin6.next()
            P.op("sp", lambda e: e.dma_start(out=xin6.v[xi], in_=x1T_s[ccg * 128:(ccg + 1) * 128, 128 + hf * DH:128 + (hf + 1) * DH]),
                 writes=[xin6.b[xi]], dsem=xin6.s[xi])
            xpre[(ccg, hf)] = xi

        def dn_epi(ccg, bk, hf):
            xi = xpre.pop((ccg, hf))
            o = ost.next()
            P.op("dve", lambda e: e.scalar_tensor_tensor(out=ost.v[o], in0=pb[bk][:, 0:DH], scalar=gate2(ccg), in1=xin6.v[xi],
                                                         op0=ALU.mult, op1=ALU.add), reads=[Bpb[bk], xin6.b[xi], Bmod], writes=[ost.b[o]])
            outs.append(P.op("sp", lambda e: e.dma_start(out=outT[ccg * 128:(ccg + 1) * 128, hf * DH:(hf + 1) * DH], in_=ost.v[o]),
                             reads=[ost.b[o]], dsem=ost.s[o]))

        NKG = FFC // KG
        g6 = max(KG, ((FFC // 8 + KG - 1) // KG) * KG)
        Bhb6g = []
        for gi, a in enumerate(range(0, FFC, g6)):
            bgi = Buf(f"hbuf6_{gi}")
            Bhb6g.append(bgi)
            P.op("sp", lambda e, a=a: e.dma_start(out=hbuf6[:, a:min(FFC, a + g6), :],
                                                  in_=actT_s.rearrange("(kc p) t -> p kc t", p=128)[:, a:min(FFC, a + g6), :]),
                 writes=[bgi], dsem=P.dsem(f"hb6_{gi}"))

        def cons_dn(i, sp, wv, wb):
            w_ap, KC, c0, ncols, (cg, kg) = sp
            if kg == 0:
                for l in range(NCC):
                    for tt in range(NTT):
                        dn_pre(cg * NCC + l, tt)
            fns = []
            for kcl in range(KG):
                for l in range(NCC):
                    for tt in range(NTT):
                        first = (kg == 0 and kcl == 0)
                        last = (kg == NKG - 1 and kcl == KG - 1)
                        fns.append(lambda e, kcl=kcl, l=l, tt=tt, first=first, last=last: e.matmul(
                            pb[l * NTT + tt][:, 0:DH], lhsT=wv[:, kcl, l * 128:(l + 1) * 128],
                            rhs=hbuf6[:, kg * KG + kcl, tt * DH:(tt + 1) * DH], start=first, stop=last))
            P.op("pe", fns, reads=[wb, Bhb6g[(kg * KG) // g6]], writes=[Bpb[i] for i in range(NCC * NTT)], acc=(kg > 0))
            if kg == NKG - 1:
                for l in range(NCC):
                    for tt in range(NTT):
                        dn_epi(cg * NCC + l, l * NTT + tt, tt)
        specs = []
        for cg in range(D // CGW):
            for kg in range(NKG):
                specs.append((w_down[kg * KG * 128:(kg + 1) * KG * 128, :], KG, cg * CGW, CGW, (cg, kg)))
        WStream(wr6, specs).run(cons_dn)
        P.run(outs)
    return nc


def host_inputs(cfg, inp, core):
    c = cfg
    b, half = core // 2, core % 2
    D, H, S, F, FC, NB, NO, NL, NT, FFC = c.D, c.H, c.S, c.F, c.FC, c.NB, c.NO, c.NL, c.NT, c.FFC
    f32 = np.float32
    x = inp["x"][b]
    xT = np.ascontiguousarray(x.T)
    if half == 1:
        loc = xT[:, (NO - 1) * 128:]
    else:
        loc = np.concatenate([xT[:, 0:128], xT[:, 0:NO * 128]], axis=1)
    colT = lambda v: np.ascontiguousarray(np.asarray(v, f32).reshape(-1, 128).T)
    cst = np.zeros((128, c.NCST), f32)

    def put(name, arr):
        o, w = c.coff[name]
        arr = np.asarray(arr, f32)
        assert arr.shape == (128, w), (name, arr.shape, w)
        cst[:, o:o + w] = arr
    put("cT", colT(inp["c"][b]))
    put("bada", colT(inp["b_ada"][0]))
    put("n1g", colT(inp["norm1_g"][0])); put("n2g", colT(inp["norm2_g"][0])); put("sgug", colT(inp["sgu_norm_g"][0]))
    put("qg", colT(inp["q_norm_g"][0])); put("kg", colT(inp["k_norm_g"][0]))
    put("lam", np.stack([inp["lambda_q1"][0], inp["lambda_q2"][0], inp["lambda_k1"][0], inp["lambda_k2"][0]], axis=1))
    put("subg", colT(inp["subln_g"][0]))
    cw = np.asarray(inp["conv_w"][0], f32)
    put("convw", np.concatenate([colT(cw[j]) for j in range(3)], axis=1))
    put("convb", colT(inp["conv_b"][0]))
    put("flag", np.full((128, 1), float(half), f32))
    put("eps", np.full((128, 1), EPS, f32))
    slopes = 2.0 ** (-8.0 * np.arange(1, H + 1, dtype=np.float64) / H)
    cb = np.full((H, NL, NB), NEG, np.float64)
    for j in range(NL):
        gq = (NO - 1 + j) if half == 1 else max(j - 1, 0)
        for kb in range(NB):
            if kb <= gq:
                cb[:, j, kb] = -slopes * 128.0 * (gq - kb) - M0
    put("cb", np.broadcast_to(cb.reshape(1, -1).astype(f32), (128, H * NL * NB)))
    kk = np.arange(128)[:, None]; qq = np.arange(128)[None, :]
    put("tri", (kk <= qq).astype(f32))
    bsp = np.asarray(inp["b_spatial"][0], f32)
    put("bsp", np.broadcast_to(bsp.reshape(1, -1), (128, c.G * 128)))
    ws = np.asarray(inp["w_spatial"][0], f32)
    put("wsT", np.ascontiguousarray(ws.transpose(2, 0, 1)).reshape(128, c.G * 128))
    import ml_dtypes
    bf = ml_dtypes.bfloat16
    SC = 128.0 ** -0.5
    abt = np.zeros((H, 4, 384), np.float32)
    kidx = np.arange(128, dtype=np.float32)
    for h in range(H):
        sg = slopes[h] / SC
        sh = np.float32(np.float32(sg).astype(bf))
        sl = np.float32(np.float32(sg - float(sh)).astype(bf))
        abt[h, 0, 0:128] = kidx; abt[h, 1, 0:128] = kidx; abt[h, 2, 0:128] = -sh; abt[h, 3, 0:128] = -sl
        abt[h, 0, 128:384] = sh; abt[h, 1, 128:384] = sl
        abt[h, 2, 128:384] = np.tile(kidx, 2); abt[h, 3, 128:384] = np.tile(kidx, 2)
    msk = np.zeros((128, 3, 256), np.float32)
    msk[:, 0, 0:128] = np.eye(128, dtype=np.float32)
    tri = np.where(qq >= kk, 0.0, NEG / SC).astype(np.float32)
    tri2 = np.concatenate([tri, tri], axis=1)
    if half == 1:
        msk[:, 1, :] = tri2
    else:
        msk[:, 2, :] = tri2
    return {
        "xT_ctx": xT, "xT_loc": np.ascontiguousarray(loc), "cst": cst, "abt": abt.astype(bf), "msk": msk.astype(bf),
        "w_ada": np.asarray(inp["w_ada"][0]), "w_in": np.asarray(inp["w_in"][0]), "w_out": np.asarray(inp["w_out"][0]),
        "w_up": np.asarray(inp["w_ff_up"][0]), "w_down": np.asarray(inp["w_ff_down"][0]),
    }


def assemble(cfg, results):
    c = cfg
    out = np.zeros((c.B, c.S, c.D), np.float32)
    for core, r in enumerate(results):
        b, half = core // 2, core % 2
        out[b, half * c.NOT:(half + 1) * c.NOT, :] = r["outT"].T
    return out


_NC_CACHE = {}


def kernel(**inputs):
    cfg = Cfg()
    inp = {k: np.asarray(v) for k, v in inputs.items()}
    if "nc" not in _NC_CACHE:
        _NC_CACHE["nc"] = build(cfg)
    nc = _NC_CACHE["nc"]
    in_maps = [host_inputs(cfg, inp, core) for core in range(8)]
    res = run_bass_kernel_spmd(nc, in_maps, core_ids=list(range(8)))
    return assemble(cfg, res.results)
```

```python
import math
from contextlib import ExitStack
import numpy as np
import concourse.bass as bass
import concourse.mybir as mybir
from concourse.bass_utils import run_bass_kernel_spmd

F32 = mybir.dt.float32
BF16 = mybir.dt.bfloat16
AF = mybir.ActivationFunctionType
ALU = mybir.AluOpType
AX = mybir.AxisListType

EPS = 1e-6
NEG = -30000.0
M0 = 8.0
LAMBDA_INIT = 0.8 - 0.6 * math.exp(-0.3 * 0)


class Cfg:
    def __init__(self, D=4096, H=16, S=2048, F=11008, B=4):
        self.D, self.H, self.S, self.F, self.B = D, H, S, F, B
        self.G = H
        self.FC = D // 128
        self.NB = S // 128
        self.NO = self.NB // 2
        self.NL = self.NO + 1
        self.NT = self.NL * 128
        self.NOT = self.NO * 128
        self.FFC = F // 128
        self.TT = 384
        self.KVH = S // 2
        self.KVT = min(512, self.KVH)
        self.N1T = min(256, S)
        self.FTOK = self.NOT + 2
        self.FT = self.FTOK // 3
        self.DH = self.NOT if self.NOT <= 512 else self.NOT // 2
        assert D == 256 * H and self.NT % self.TT == 0 and self.FTOK % 3 == 0
        assert F % 256 == 0 and self.NOT % self.DH == 0
        o = {}
        n = 0
        for name, w in [("cT", self.FC), ("bada", 6 * self.FC), ("n1g", self.FC), ("n2g", self.FC),
                        ("sgug", self.FC), ("qg", 1), ("kg", 1), ("lam", 4), ("subg", 2),
                        ("convw", 3 * 2 * self.FFC), ("convb", 2 * self.FFC), ("flag", 1), ("eps", 1),
                        ("cb", self.H * self.NL * self.NB), ("tri", 128), ("bsp", self.G * 128),
                        ("wsT", self.G * 128)]:
            o[name] = (n, w)
            n += w
        self.coff = o
        self.NCST = n


class Sem:
    def __init__(self, h, name):
        self.h, self.name, self.count = h, name, 0


class Buf:
    __slots__ = ("name", "last_w", "readers")

    def __init__(self, name):
        self.name, self.last_w, self.readers = name, None, {}


class Op:
    __slots__ = ("eng", "fns", "deps", "sem", "val", "needs_sig", "is_dma", "idx")


class Prog:
    ENG = ("pe", "act", "dve", "pool", "sp")

    def __init__(self, nc, stack):
        self.nc, self.stack = nc, stack
        self.q = {e: [] for e in self.ENG}
        self.nops = 0
        self.dsems = []
        self.fence = None
        self.fence_seen = set()

    def sem(self, name):
        return Sem(self.stack.enter_context(self.nc.semaphore(name)), name)

    def dsem(self, name):
        s = self.sem(name)
        self.dsems.append(s)
        return s

    def op(self, eng, fns, reads=(), writes=(), dsem=None, deps=(), acc=False):
        if callable(fns):
            fns = [fns]
        o = Op()
        o.eng, o.fns, o.is_dma = eng, fns, dsem is not None
        o.sem = o.val = None
        o.needs_sig = False
        o.idx = self.nops
        self.nops += 1
        dd = {}
        for d in deps:
            if d is not None:
                dd[id(d)] = d
        if self.fence is not None and eng not in self.fence_seen:
            self.fence_seen.add(eng)
            dd[id(self.fence)] = self.fence
        for b in reads:
            if b.last_w is not None:
                dd[id(b.last_w)] = b.last_w
        for b in writes:
            if b.last_w is not None and not (acc and b.last_w.eng == eng and not b.last_w.is_dma):
                dd[id(b.last_w)] = b.last_w
            for r in b.readers.values():
                dd[id(r)] = r
        o.deps = list(dd.values())
        for d in o.deps:
            d.needs_sig = True
        key = ("dma", o.idx) if o.is_dma else eng
        for b in reads:
            b.readers[key] = o
        for b in writes:
            b.last_w = o
            b.readers = {}
        if o.is_dma:
            o.sem = dsem
            dsem.count += 16 * len(fns)
            o.val = dsem.count
            o.needs_sig = True
        self.q[eng].append(o)
        return o

    def barrier(self):
        deps = []
        for e in self.ENG:
            for o in reversed(self.q[e]):
                if not o.is_dma:
                    deps.append(o)
                    break
        for s in self.dsems:
            if s.count > 0:
                f = Op()
                f.sem, f.val, f.needs_sig, f.is_dma = s, s.count, True, True
                deps.append(f)
        b = self.op("sp", lambda e: e.nop(), deps=deps)
        b.needs_sig = True
        self.fence = b
        self.fence_seen = {"sp"}
        return b

    def _assign(self):
        self.esem = {}
        for eng in self.ENG:
            s = None
            for o in self.q[eng]:
                if o.is_dma or not o.needs_sig:
                    continue
                if s is None:
                    s = self.sem("e_" + eng)
                s.count += 1
                o.sem, o.val = s, s.count

    def replay(self, eng, e):
        seen = {}
        for o in self.q[eng]:
            for d in o.deps:
                s, v = d.sem, d.val
                if seen.get(s.name, 0) >= v:
                    continue
                seen[s.name] = v
                e.wait_ge(s.h, v)
            last = len(o.fns) - 1
            for i, fn in enumerate(o.fns):
                ins = fn(e)
                if o.is_dma:
                    ins.then_inc(o.sem.h, 16)
                elif o.needs_sig and i == last:
                    ins.then_inc(o.sem.h, 1)

    def run(self, final):
        nc = self.nc
        for o in final:
            o.needs_sig = True
        self._assign()
        with nc.Block() as block:
            @block.tensor
            def _(e):
                self.replay("pe", e)

            @block.scalar
            def _(e):
                self.replay("act", e)

            @block.vector
            def _(e):
                self.replay("dve", e)

            @block.gpsimd
            def _(e):
                self.replay("pool", e)

            @block.sync
            def _(e):
                self.replay("sp", e)
                seen = {}
                for o in final:
                    seen[o.sem.name] = max(seen.get(o.sem.name, (0, None))[0], o.val), o.sem
                for v, s in seen.values():
                    e.wait_ge(s.h, v)


class Arena:
    def __init__(self, ap, nwords):
        self.ap, self.n, self.top = ap, nwords, 0

    def alloc(self, shape, dt):
        n = 1
        for s in shape[1:]:
            n *= s
        words = n if dt == F32 else (n + 1) // 2
        off = self.top
        self.top += words
        assert self.top <= self.n, f"arena overflow {self.top} > {self.n}"
        v = self.ap[:, off:off + words]
        if dt != F32:
            v = v.bitcast(dt)[:, 0:n]
        if len(shape) == 3:
            v = v.rearrange("p (a b) -> p a b", a=shape[1])
        elif len(shape) == 4:
            v = v.rearrange("p (a b c) -> p a b c", a=shape[1], b=shape[2])
        return v


class Ring:
    def __init__(self, P, A, name, n, shape, dt, with_sem=True):
        self.v = [A.alloc(shape, dt) for _ in range(n)]
        self.b = [Buf(f"{name}{i}") for i in range(n)]
        self.s = [P.dsem(f"{name}{i}") for i in range(n)] if with_sem else None
        self.n, self.i = n, 0

    def next(self):
        k = self.i % self.n
        self.i += 1
        return k


def build(cfg):
    c = cfg
    D, H, S, F, FC, NB, NO, NL, NT, FFC = c.D, c.H, c.S, c.F, c.FC, c.NB, c.NO, c.NL, c.NT, c.FFC
    nc = bass.Bass("TRN2", target_bir_lowering=False)
    dt_in = lambda n, s, d=F32: nc.dram_tensor(n, s, d, kind="ExternalInput").ap()
    scr = lambda n, s, d=BF16: nc.dram_tensor(n, s, d, kind="Internal").ap()
    xT_ctx = dt_in("xT_ctx", [D, S])
    xT_loc = dt_in("xT_loc", [D, NT])
    cst_d = dt_in("cst", [128, c.NCST])
    abt_d = dt_in("abt", [H, 4, 384], BF16)
    msk_d = dt_in("msk", [128, 3, 256], BF16)
    w_ada = dt_in("w_ada", [D, 6 * D])
    w_in = dt_in("w_in", [D, 7 * D])
    w_out = dt_in("w_out", [D, D])
    w_up = dt_in("w_up", [D, 2 * F])
    w_down = dt_in("w_down", [F, D])
    outT = nc.dram_tensor("outT", [D, c.NOT], F32, kind="ExternalOutput").ap()
    hT_ctx = scr("hT_ctx", [D, S]); hT_loc = scr("hT_loc", [D, NT])
    kT_s = scr("kT_s", [D, S]); vB_s = scr("vB_s", [S, D]); qT_s = scr("qT_s", [D, NT])
    uT_s = scr("uT_s", [D, NT]); gA_s = scr("gA_s", [D, NT]); gB_s = scr("gB_s", [D, NT])
    vA_s = scr("vA_s", [NT, D]); yT_s = scr("yT_s", [D, NT]); x1T_s = scr("x1T_s", [D, NT], F32)
    actT_s = scr("actT_s", [F, c.NOT])

    fcv = lambda ap: ap.rearrange("(fc p) t -> p fc t", p=128)

    with ExitStack() as st:
        P = Prog(nc, st)
        AW = 52224
        arena_t = st.enter_context(nc.sbuf_tensor("arena", [128, AW], F32))
        A = Arena(arena_t[:, :], AW)
        pb = [st.enter_context(nc.psum_tensor(f"pb{i}", [128, 512], F32)) for i in range(8)]
        Bpb = [Buf(f"pb{i}") for i in range(8)]

        modT = A.alloc([128, 6 * FC], F32); Bmod = Buf("mod")
        s1c = A.alloc([128, FC], F32); s2c = A.alloc([128, FC], F32)
        ones_f = A.alloc([128, 128], F32); ones_b = A.alloc([128, 128], BF16); Bones = Buf("ones")
        scb = A.alloc([128, FC], BF16); Bsc = Buf("sc")
        neglam = A.alloc([128, 1], F32); Blam = Buf("lam")
        subc = A.alloc([128, 2], F32)
        rvA = A.alloc([128, NL], F32); BrvA = Buf("rvA")
        PERSIST_SMALL = A.top
        cst = A.alloc([128, c.NCST], F32)
        Bcst = Buf("cst")
        s_cst = P.dsem("cst")
        P.op("sp", lambda e: e.dma_start(out=cst, in_=cst_d), writes=[Bcst], dsem=s_cst)

        def cc(name, i=0, w=1):
            o, _ = c.coff[name]
            return cst[:, o + i:o + i + w]

        wsTm = A.alloc([128, c.G, 128], F32); Bws = Buf("wsTm")
        P.op("pool", lambda e: e.memset(ones_f, 1.0), writes=[Bones])
        P.op("pool", lambda e: e.memset(ones_b, 1.0), writes=[Bones])
        PERSIST = A.top
        rstd_all = A.alloc([128, S + NT], F32); Brstd = Buf("rstd")
        PERSIST_N = A.top

        shift1 = lambda i: modT[:, 0 * FC + i:0 * FC + i + 1]
        gate1 = lambda i: modT[:, 2 * FC + i:2 * FC + i + 1]
        shift2 = lambda i: modT[:, 3 * FC + i:3 * FC + i + 1]
        gate2 = lambda i: modT[:, 5 * FC + i:5 * FC + i + 1]
        epsc = cc("eps")

        class WStream:
            def __init__(self, ring, specs):
                self.ring, self.specs, self.issued = ring, specs, 0

            def issue(self, upto):
                while self.issued <= min(upto, len(self.specs) - 1):
                    i = self.issued
                    w_ap, KC, c0, ncols = self.specs[i][:4]
                    k = i % self.ring.n
                    dst = self.ring.v[k][:, 0:KC, 0:ncols]
                    src = w_ap.rearrange("(kc p) c -> p kc c", p=128)[:, :, c0:c0 + ncols]
                    P.op("pool", lambda e, dst=dst, src=src: e.dma_start(out=dst, in_=src),
                         writes=[self.ring.b[k]], dsem=self.ring.s[k])
                    self.issued += 1

            def run(self, consume):
                for i, sp in enumerate(self.specs):
                    self.issue(i + self.ring.n - 1)
                    k = i % self.ring.n
                    consume(i, sp, self.ring.v[k], self.ring.b[k])

        pending = []

        def flush_pending(keep=0):
            while len(pending) > keep:
                pending.pop(0)()

        def tiles_of(n, tw):
            return [(t0, min(tw, n - t0)) for t0 in range(0, n, tw)]

        bank_rr = [0]
        qk_aux = [0]

        def gemm_bank(nb):
            k = bank_rr[0] % nb
            bank_rr[0] += 1
            return k

        def mm_group(psap, pairs, reads, bankbuf, skip=False):
            n = len(pairs)
            fns = [(lambda e, l=l, r=r, i=i: e.matmul(psap, lhsT=l, rhs=r, start=(i == 0), stop=(i == n - 1)))
                   for i, (l, r) in enumerate(pairs)]
            return P.op("pe", fns, reads=reads, writes=[bankbuf], acc=True)

        wr0 = Ring(P, A, "w0", 3, [128, FC, 512], BF16)
        xq = Ring(P, A, "xq", 6, [128, 512], F32)
        sqr = Ring(P, A, "sq", 4, [128, 512], F32, with_sem=False)
        r1 = A.alloc([128, 512], F32); Br1 = Buf("r1")
        P.op("act", lambda e: e.activation(out=scb, in_=cc("cT", 0, FC), func=AF.Silu), reads=[Bcst], writes=[Bsc])

        def norm_units():
            jobs = [(xT_ctx, 0, t0, tw) for (t0, tw) in tiles_of(S, 512)] + [(xT_loc, S, t0, tw) for (t0, tw) in tiles_of(NT, 512)]
            for ji, (src, off, t0, tw) in enumerate(jobs):
                bk = 2 + ji % 2
                for fc in range(FC):
                    def unit(src=src, off=off, t0=t0, tw=tw, bk=bk, fc=fc):
                        k = xq.next()
                        P.op("sp", lambda e: e.dma_start(out=xq.v[k][:, 0:tw], in_=src[fc * 128:(fc + 1) * 128, t0:t0 + tw]),
                             writes=[xq.b[k]], dsem=xq.s[k])
                        q = sqr.next()
                        P.op("act", lambda e: e.activation(out=sqr.v[q][:, 0:tw], in_=xq.v[k][:, 0:tw], func=AF.Square),
                             reads=[xq.b[k]], writes=[sqr.b[q]])
                        P.op("pe", lambda e: e.matmul(pb[bk][:, 0:tw], lhsT=ones_f, rhs=sqr.v[q][:, 0:tw], start=(fc == 0), stop=(fc == FC - 1)),
                             reads=[sqr.b[q], Bones], writes=[Bpb[bk]], acc=True)
                        if fc == FC - 1:
                            P.op("act", lambda e: e.activation(out=r1[:, 0:tw], in_=pb[bk][:, 0:tw], func=AF.Sqrt, bias=epsc, scale=1.0 / D),
                                 reads=[Bpb[bk], Bcst], writes=[Br1])
                            P.op("dve", lambda e: e.reciprocal(out=rstd_all[:, off + t0:off + t0 + tw], in_=r1[:, 0:tw]),
                                 reads=[Br1], writes=[Brstd], acc=True)
                    yield unit
        units = list(norm_units())
        specs = [(w_ada, FC, p * 512, 512) for p in range(2 * D // 512)]
        upp = (len(units) + len(specs) - 1) // len(specs)

        def cons0(i, sp, wv, wb):
            for l in range(4):
                j = i * 4 + l
                mm_group(pb[0][:, j:j + 1], [(wv[:, kc, l * 128:(l + 1) * 128], scb[:, kc:kc + 1]) for kc in range(FC)],
                         [wb, Bsc], Bpb[0])
            for _ in range(upp):
                if units:
                    units.pop(0)()
        WStream(wr0, specs).run(cons0)
        while units:
            units.pop(0)()
        P.op("dve", lambda e: e.tensor_tensor(out=modT[:, 0:2 * FC], in0=pb[0][:, 0:2 * FC], in1=cc("bada", 0, 2 * FC), op=ALU.add),
             reads=[Bpb[0], Bcst], writes=[Bmod])
        P.op("dve", lambda e: e.scalar_tensor_tensor(out=s1c, in0=modT[:, FC:2 * FC], scalar=1.0, in1=cc("n1g", 0, FC),
                                                     op0=ALU.add, op1=ALU.mult), reads=[Bmod, Bcst], writes=[Bmod])
        lp = A.alloc([128, 2], F32); le = A.alloc([128, 2], F32); lt = A.alloc([128, 1], F32)
        P.op("dve", lambda e: e.tensor_tensor(out=lp, in0=cc("lam", 0, 2), in1=cc("lam", 2, 2), op=ALU.mult),
             reads=[Bcst], writes=[Blam])
        P.op("pe", lambda e: e.matmul(pb[1][:, 0:2], lhsT=ones_f, rhs=lp, start=True, stop=True),
             reads=[Blam, Bones], writes=[Bpb[1]])
        P.op("act", lambda e: e.activation(out=le, in_=pb[1][:, 0:2], func=AF.Exp), reads=[Bpb[1]], writes=[Blam])
        P.op("dve", lambda e: e.tensor_tensor(out=lt, in0=le[:, 1:2], in1=le[:, 0:1], op=ALU.subtract),
             reads=[Blam], writes=[Blam])
        P.op("dve", lambda e: e.tensor_scalar_add(out=neglam, in0=lt, scalar1=-LAMBDA_INIT), reads=[Blam], writes=[Blam])
        P.op("dve", lambda e: e.tensor_scalar_mul(out=subc, in0=cc("subg", 0, 2), scalar1=1.0 - LAMBDA_INIT),
             reads=[Bcst], writes=[Blam])
        P.op("dve", lambda e: e.tensor_tensor(out=wsTm, in0=cc("wsT", 0, c.G * 128).rearrange("p (g t) -> p g t", g=c.G),
                                              in1=cc("tri", 0, 128)[:, None, :].to_broadcast([128, c.G, 128]), op=ALU.mult),
             reads=[Bcst], writes=[Bws])
        P.barrier()

        A.top = PERSIST_N
        hbuf2 = A.alloc([128, FC, NT], BF16); Bhb2 = [Buf("hbuf2a"), Buf("hbuf2b"), Buf("hbuf2c")]; s_hb = P.dsem("hb")
        wr = Ring(P, A, "w2", 3, [128, FC, 256], BF16)
        stg = Ring(P, A, "stg", 4, [128, 512], BF16)
        sqb = Ring(P, A, "sqb", 2, [128, 512], BF16, with_sem=False)
        rq = Ring(P, A, "rq", 2, [128, 512], F32, with_sem=False)
        rq2 = Ring(P, A, "rq2", 2, [128, 512], F32, with_sem=False)
        junk = A.alloc([128, 256], BF16); Bjunk = Buf("junk")
        NPV = D // 256
        ssq = A.alloc([128, NL, NPV], F32); Bssq = Buf("ssq")
        P.op("pool", lambda e: e.memset(ssq, 0.0), writes=[Bssq])

        def load_hbuf(hb, Bh, src, t0, ntok):
            g = max(1, FC // 4)
            fns = [(lambda e, a=a: e.dma_start(out=hb[:, a:a + g, 0:ntok], in_=fcv(src)[:, a:a + g, t0:t0 + ntok]))
                   for a in range(0, FC, g)]
            P.op("sp", fns, writes=[Bh], dsem=s_hb)

        xin2 = Ring(P, A, "xin2", 8, [128, 512], F32)

        def fill_hbuf(hb, Bh, src, t0, ntok, roff, tilew):
            for ti, (a, tw) in enumerate(tiles_of(ntok, tilew)):
                for fc in range(FC):
                    def one(fc=fc, a=a, tw=tw, ti=ti):
                        k = xin2.next()
                        P.op("sp", lambda e: e.dma_start(out=xin2.v[k][:, 0:tw], in_=src[fc * 128:(fc + 1) * 128, t0 + a:t0 + a + tw]),
                             writes=[xin2.b[k]], dsem=xin2.s[k])
                        P.op("dve", lambda e: e.scalar_tensor_tensor(out=xin2.v[k][:, 0:tw], in0=xin2.v[k][:, 0:tw], scalar=s1c[:, fc:fc + 1],
                                                                     in1=rstd_all[:, roff + a:roff + a + tw], op0=ALU.mult, op1=ALU.mult),
                             reads=[Brstd, Bmod], writes=[xin2.b[k]])
                        P.op("act", lambda e: e.activation(out=hb[:, fc, a:a + tw], in_=xin2.v[k][:, 0:tw], func=AF.Identity, bias=shift1(fc), scale=1.0),
                             reads=[xin2.b[k], Bmod], writes=[Bh[ti]], acc=True)
                    one()

        ada_left = [(w_ada, FC, 2 * D + p * 256, 256, ("ada", p)) for p in range(3 * D // 256)]
        ada_ctr = [0]

        def cons_ada(sp, wv, wb):
            p = sp[4][1]
            for l in range(2):
                j = p * 2 + l
                mm_group(pb[6][:, j:j + 1], [(wv[:, kc, l * 128:(l + 1) * 128], scb[:, kc:kc + 1]) for kc in range(FC)], [wb, Bsc], Bpb[6])

        def with_ada(specs, cons):
            out = []
            for sp in specs:
                out.append(sp)
                ada_ctr[0] += 1
                if ada_left and ada_ctr[0] % 3 == 0:
                    out.append(ada_left.pop(0))

            def f(i, sp, wv, wb):
                if isinstance(sp[4], tuple) and sp[4][0] == "ada":
                    cons_ada(sp, wv, wb)
                else:
                    cons(i, sp, wv, wb)
            return out, f

        def run_ada(specs, cons):
            sp2, c2 = with_ada(specs, cons)
            WStream(wr, sp2).run(c2)

        def epi_simple(func, dst):
            def f(sp, l, ccg, t0, tw, ps, bb, tg0):
                k = stg.next()
                P.op("act", lambda e: e.activation(out=stg.v[k][:, 0:tw], in_=ps[:, 0:tw], func=func), reads=[bb], writes=[stg.b[k]])
                P.op("sp", lambda e: e.dma_start(out=dst[ccg * 128:(ccg + 1) * 128, tg0 + t0:tg0 + t0 + tw], in_=stg.v[k][:, 0:tw]),
                     reads=[stg.b[k]], dsem=stg.s[k])
            return f

        def epi_qk(gname, dst):
            def f(sp, l, ccg, t0, tw, ps, bb, tg0):
                a = sqb.next()
                P.op("act", lambda e: e.activation(out=sqb.v[a][:, 0:tw], in_=ps[:, 0:tw], func=AF.Square), reads=[bb], writes=[sqb.b[a]])

                def rest():
                    xb = 4 + qk_aux[0] % 2
                    qk_aux[0] += 1
                    P.op("pe", lambda e: e.matmul(pb[xb][:, 0:tw], lhsT=ones_b, rhs=sqb.v[a][:, 0:tw], start=True, stop=True),
                         reads=[sqb.b[a], Bones], writes=[Bpb[xb]])
                    r = rq.next()
                    P.op("act", lambda e: e.activation(out=rq.v[r][:, 0:tw], in_=pb[xb][:, 0:tw], func=AF.Sqrt, bias=epsc, scale=1.0 / 128),
                         reads=[Bpb[xb], Bcst], writes=[rq.b[r]])
                    r2 = rq2.next()
                    P.op("dve", lambda e: e.reciprocal(out=rq2.v[r2][:, 0:tw], in_=rq.v[r][:, 0:tw]), reads=[rq.b[r]], writes=[rq2.b[r2]])
                    k = stg.next()
                    P.op("dve", lambda e: e.scalar_tensor_tensor(out=stg.v[k][:, 0:tw], in0=ps[:, 0:tw], scalar=cc(gname), in1=rq2.v[r2][:, 0:tw],
                                                                 op0=ALU.mult, op1=ALU.mult), reads=[bb, rq2.b[r2], Bcst], writes=[stg.b[k]])
                    P.op("sp", lambda e: e.dma_start(out=dst[ccg * 128:(ccg + 1) * 128, tg0 + t0:tg0 + t0 + tw], in_=stg.v[k][:, 0:tw]),
                         reads=[stg.b[k]], dsem=stg.s[k])
                pending.append(rest)
            return f

        def cons_F(hb, Bh, tiles, epi, tg0, nbank=4):
            def f(i, sp, wv, wb):
                w_ap, KC, c0, ncols, ccg0 = sp
                for l in range(ncols // 128):
                    for ti, (t0, tw) in enumerate(tiles):
                        bk = gemm_bank(nbank)
                        mm_group(pb[bk][:, 0:tw], [(wv[:, kc, l * 128:(l + 1) * 128], hb[:, kc, t0:t0 + tw]) for kc in range(KC)],
                                 ([wb] + list(Bh)) if isinstance(Bh, tuple) else [wb, Bh[ti] if isinstance(Bh, list) else Bh], Bpb[bk])
                        flush_pending()
                        epi(sp, l, ccg0 + l, t0, tw, pb[bk], Bpb[bk], tg0)
            return f

        def cons_T(hb, Bh, nblk, kind, tg0, tilew):
            def f(i, sp, wv, wb):
                w_ap, KC, c0, ncols, pidx = sp
                for tb in range(nblk):
                    bk = gemm_bank(4)
                    mm_group(pb[bk][:, 0:ncols], [(hb[:, kc, tb * 128:(tb + 1) * 128], wv[:, kc, 0:ncols]) for kc in range(KC)],
                             [wb, Bh[(tb * 128) // tilew]], Bpb[bk])
                    k = stg.next()
                    sv = stg.v[k][:, 0:ncols]
                    if kind == "vB":
                        P.op("dve", lambda e, sv=sv, bk=bk: e.tensor_copy(out=sv, in_=pb[bk][:, 0:ncols]), reads=[Bpb[bk]], writes=[stg.b[k]])
                        dst = vB_s[tg0 + tb * 128:tg0 + (tb + 1) * 128, pidx * 256:pidx * 256 + ncols]
                    else:
                        P.op("act", lambda e, sv=sv, bk=bk: e.activation(out=sv, in_=pb[bk][:, 0:ncols], func=AF.Gelu), reads=[Bpb[bk]], writes=[stg.b[k]])
                        P.op("act", lambda e, sv=sv, tb=tb, pidx=pidx: e.activation(out=junk[:, 0:ncols], in_=sv, func=AF.Square,
                                                                                  accum_out=ssq[:, tb, pidx:pidx + 1]),
                             reads=[stg.b[k]], writes=[Bjunk, Bssq])
                        dst = vA_s[tb * 128:(tb + 1) * 128, pidx * 256:pidx * 256 + ncols]
                    P.op("sp", lambda e, sv=sv, dst=dst: e.dma_start(out=dst, in_=sv), reads=[stg.b[k]], dsem=stg.s[k])
            return f

        OU, OVA, OQ, OK_, OVB, OGA, OGB = [i * D for i in range(7)]
        for half in range(2):
            fill_hbuf(hbuf2, Bhb2, xT_ctx, half * c.KVH, c.KVH, half * c.KVH, c.KVT)
            specs = [(w_in, FC, OK_ + p * 256, 256, p * 2) for p in range(NPV)]
            run_ada(specs, cons_F(hbuf2, Bhb2, tiles_of(c.KVH, c.KVT), epi_qk("kg", kT_s), half * c.KVH))
            flush_pending()
            specs = [(w_in, FC, OVB + p * 256, 256, p) for p in range(NPV)]
            run_ada(specs, cons_T(hbuf2, Bhb2, c.KVH // 128, "vB", half * c.KVH, c.KVT))
        fill_hbuf(hbuf2, Bhb2, xT_loc, 0, NT, S, c.TT)
        mt = [(126 + i * c.FT, c.FT) for i in range(3)]
        zt = A.alloc([128, 128], BF16); Bzt = Buf("zt"); s_zt = P.dsem("zt")
        P.op("pool", lambda e: e.memset(zt, 0.0), writes=[Bzt])
        P.op("sp", [(lambda e, d=d: e.dma_start(out=fcv(d)[:, :, 0:126], in_=zt[:, None, 0:126].to_broadcast([128, FC, 126])))
                    for d in (uT_s, gA_s, gB_s, qT_s)], reads=[Bzt], dsem=s_zt)
        run_ada([(w_in, FC, OVA + p * 256, 256, p) for p in range(NPV)], cons_T(hbuf2, Bhb2, NL, "vA", 0, c.TT))
        run_ada([(w_in, FC, OU + p * 256, 256, p * 2) for p in range(NPV)], cons_F(hbuf2, tuple(Bhb2), mt, epi_simple(AF.Gelu, uT_s), 0))
        run_ada([(w_in, FC, OGA + p * 256, 256, p * 2) for p in range(NPV)], cons_F(hbuf2, tuple(Bhb2), mt, epi_simple(AF.Sigmoid, gA_s), 0))
        run_ada([(w_in, FC, OGB + p * 256, 256, p * 2) for p in range(NPV)], cons_F(hbuf2, tuple(Bhb2), mt, epi_simple(AF.Sigmoid, gB_s), 0))
        run_ada([(w_in, FC, OQ + p * 256, 256, p * 2) for p in range(NPV)], cons_F(hbuf2, tuple(Bhb2), mt, epi_qk("qg", qT_s), 0))
        flush_pending()
        if ada_left:
            rest = list(ada_left)
            del ada_left[:]
            WStream(wr, rest).run(lambda i, sp, wv, wb: cons_ada(sp, wv, wb))
        P.op("dve", lambda e: e.tensor_tensor(out=modT[:, 2 * FC:5 * FC], in0=pb[6][:, 0:3 * FC], in1=cc("bada", 2 * FC, 3 * FC), op=ALU.add),
             reads=[Bpb[6], Bcst], writes=[Bmod])
        P.op("dve", lambda e: e.scalar_tensor_tensor(out=s2c, in0=modT[:, 4 * FC:5 * FC], scalar=1.0, in1=cc("n2g", 0, FC),
                                                     op0=ALU.add, op1=ALU.mult), reads=[Bmod, Bcst], writes=[Bmod])
        sst = A.alloc([128, NL], F32); sst2 = A.alloc([128, NL], F32)
        P.op("dve", lambda e: e.tensor_reduce(out=sst, in_=ssq, axis=AX.X, op=ALU.add), reads=[Bssq], writes=[BrvA])
        P.op("act", lambda e: e.activation(out=sst2, in_=sst, func=AF.Sqrt, bias=epsc, scale=1.0 / D), reads=[BrvA, Bcst], writes=[BrvA])
        P.op("dve", lambda e: e.reciprocal(out=rvA, in_=sst2), reads=[BrvA], writes=[BrvA])
        P.barrier()

        A.top = PERSIST
        hs_k = [A.alloc([128, 2, S], BF16) for _ in range(2)]
        hs_q = [A.alloc([128, 2, NT], BF16) for _ in range(2)]
        hs_v = [A.alloc([128, NB, 256], BF16) for _ in range(2)]
        hs_va = [A.alloc([128, NL, 256], BF16) for _ in range(2)]
        hs_u = [A.alloc([128, 2, NT], BF16) for _ in range(2)]
        hs_ga = [A.alloc([128, 2, NT], BF16) for _ in range(2)]
        hs_gb = [A.alloc([128, 2, NT], BF16) for _ in range(2)]
        hs_ab = [A.alloc([128, 384], BF16) for _ in range(2)]
        mskt = A.alloc([128, 3, 256], BF16); Bmsk = Buf("msk"); s_msk = P.dsem("msk")
        P.op("sp", lambda e: e.dma_start(out=mskt, in_=msk_d), writes=[Bmsk], dsem=s_msk)
        Bhs = [Buf("hs0"), Buf("hs1")]; s_hs = [P.dsem("hs0"), P.dsem("hs1")]
        yth = Ring(P, A, "yth", 2, [128, 2, NT], BF16)
        yAg2 = [A.alloc([128, 2, NT], BF16) for _ in range(2)]; ByA2 = [Buf("yAg0"), Buf("yAg1")]
        oh2 = [A.alloc([128, 2, NT], F32) for _ in range(2)]; Boh2 = [Buf("oh0"), Buf("oh1")]
        sqh = A.alloc([128, 2, NT], BF16); Bsqh = Buf("sqh")
        rsub = A.alloc([128, NT], F32); rsub2 = A.alloc([128, NT], F32); Brsub = Buf("rsub")
        wsj = Ring(P, A, "wsj", 3, [128, 128], BF16, with_sem=False)
        mix = Ring(P, A, "mix", 2, [128, 512], F32, with_sem=False)
        pTr = Ring(P, A, "pT", 6, [128, 256], BF16, with_sem=False)
        S4 = [0, 1, 6, 7]
        rl = A.alloc([128, 256], F32); Brl = Buf("rl")
        onn = A.alloc([128, 2, 2, 128], F32); Bon = Buf("on")
        hrow = lambda ap, h: ap.rearrange("(hc p) t -> p hc t", p=128)[:, 2 * h:2 * h + 2, :]
        SCALE = 128.0 ** -0.5

        def load_head(h):
            k = h % 2
            fns = [
                lambda e: e.dma_start(out=hs_k[k], in_=hrow(kT_s, h)),
                lambda e: e.dma_start(out=hs_q[k], in_=hrow(qT_s, h)),
                lambda e: e.dma_start(out=hs_v[k], in_=vB_s.rearrange("(kb p) c -> p kb c", p=128)[:, :, h * 256:(h + 1) * 256]),
                lambda e: e.dma_start(out=hs_va[k], in_=vA_s.rearrange("(kb p) c -> p kb c", p=128)[:, :, h * 256:(h + 1) * 256]),
                lambda e: e.dma_start(out=hs_u[k], in_=hrow(uT_s, h)),
                lambda e: e.dma_start(out=hs_ga[k], in_=hrow(gA_s, h)),
                lambda e: e.dma_start(out=hs_gb[k], in_=hrow(gB_s, h)),
                lambda e: e.dma_start(out=hs_ab[k][0:4, :], in_=abt_d[h]),
            ]
            P.op("sp", fns, writes=[Bhs[k]], dsem=s_hs[k])

        aux_i = [0]
        sb_i = [0]
        acc_i = [0]
        load_head(0)

        def sgu_group(h, k, j0):
            js = list(range(j0, min(NL, j0 + 4)))
            nb_ = len(js)
            sl = slice(j0 * 128, (j0 + nb_) * 128)
            wks = []
            for j in js:
                wk = wsj.next()
                P.op("dve", lambda e, wk=wk, j=j: e.tensor_scalar_mul(out=wsj.v[wk], in0=wsTm[:, h, :], scalar1=rvA[:, j:j + 1]),
                     reads=[Bws, BrvA], writes=[wsj.b[wk]])
                wks.append(wk)

            def chunk(e2):
                xb = S4[sb_i[0] % 4]
                sb_i[0] += 1
                fns = [(lambda e, j=j, wk=wk: e.matmul(pb[xb][:, (j - j0) * 128:(j - j0 + 1) * 128],
                                                       lhsT=hs_va[k][:, j, e2 * 128:(e2 + 1) * 128], rhs=wsj.v[wk],
                                                       start=True, stop=True)) for j, wk in zip(js, wks)]
                P.op("pe", fns, reads=[Bhs[k]] + [wsj.b[w] for w in wks], writes=[Bpb[xb]])
                m = mix.next()
                mv = mix.v[m][:, 0:nb_ * 128]
                P.op("dve", lambda e: e.scalar_tensor_tensor(
                    out=mv.rearrange("p (a b) -> p a b", a=nb_),
                    in0=pb[xb][:, 0:nb_ * 128].rearrange("p (a b) -> p a b", a=nb_),
                    scalar=cc("sgug", 2 * h + e2), in1=cc("bsp", h * 128, 128)[:, None, :].to_broadcast([128, nb_, 128]),
                    op0=ALU.mult, op1=ALU.add), reads=[Bpb[xb], Bcst], writes=[mix.b[m]])
                P.op("pool", lambda e: e.tensor_tensor(out=mv, in0=mv, in1=hs_u[k][:, e2, sl], op=ALU.mult),
                     reads=[Bhs[k]], writes=[mix.b[m]])
                P.op("pool", lambda e: e.tensor_tensor(out=yAg2[k][:, e2, sl], in0=mv, in1=hs_ga[k][:, e2, sl], op=ALU.mult),
                     reads=[Bhs[k], mix.b[m]], writes=[ByA2[k]], acc=True)
            for e2 in range(2):
                chunk(e2)

        def attn_step(h, k, j, kb, nkb, ob, lb):
            kbB = (j - 1) if j >= 1 else 0
            rsel = 1 if kb == NO - 1 + j else (2 if kb == kbB else 0)
            sbk = S4[sb_i[0] % 4]
            sb_i[0] += 1
            fns = [(lambda e, cm=cm: e.matmul(pb[sbk][:, cm * 128:(cm + 1) * 128],
                                              lhsT=hs_k[k][:, cm, kb * 128:(kb + 1) * 128],
                                              rhs=hs_q[k][:, cm, j * 128:(j + 1) * 128], start=(cm == 0), stop=False, skip_group_check=True))
                   for cm in range(2)]
            fns.append(lambda e: e.matmul(pb[sbk][:, 0:256], lhsT=hs_ab[k][0:4, 0:128], rhs=hs_ab[k][0:4, 128:384],
                                          start=False, stop=(rsel == 0), skip_group_check=True))
            if rsel:
                fns.append(lambda e: e.matmul(pb[sbk][:, 0:256], lhsT=mskt[:, 0, 0:128], rhs=mskt[:, rsel, :],
                                              start=False, stop=True, skip_group_check=True))
            P.op("pe", fns, reads=[Bhs[k], Bmsk], writes=[Bpb[sbk]])
            flush_pending(keep=2)
            pt = pTr.next()
            cbi = (h * NL + j) * NB + kb
            P.op("act", lambda e: e.activation(out=pTr.v[pt], in_=pb[sbk][:, 0:256], func=AF.Exp, bias=cc("cb", cbi), scale=SCALE),
                 reads=[Bpb[sbk], Bcst], writes=[pTr.b[pt]])

            def pv():
                first, last = kb == 0, kb == nkb - 1
                fns = [
                    lambda e: e.matmul(pb[ob][:, 0:256], lhsT=hs_v[k][:, kb, 0:128], rhs=pTr.v[pt], start=first, stop=last, skip_group_check=True),
                    lambda e: e.matmul(pb[ob][:, 256:512], lhsT=hs_v[k][:, kb, 128:256], rhs=pTr.v[pt], start=False, stop=last, skip_group_check=True),
                    lambda e: e.matmul(pb[lb][:, 0:256], lhsT=ones_b, rhs=pTr.v[pt], start=first, stop=last),
                ]
                P.op("pe", fns, reads=[Bhs[k], pTr.b[pt], Bones], writes=[Bpb[ob], Bpb[lb]], acc=True)
            pending.append(pv)

        def attn_block(h, k, j):
            ob = 2 + acc_i[0] % 2
            lb = 4 + acc_i[0] % 2
            acc_i[0] += 1
            nkb = NO + j
            for kb in range(nkb):
                attn_step(h, k, j, kb, nkb, ob, lb)
            flush_pending()
            P.op("dve", lambda e: e.reciprocal(out=rl, in_=pb[lb][:, 0:256]), reads=[Bpb[lb]], writes=[Brl])
            P.op("dve", lambda e: e.tensor_tensor(out=onn.rearrange("p a b c -> p a (b c)"),
                                                  in0=pb[ob][:, 0:512].rearrange("p (a b) -> p a b", a=2),
                                                  in1=rl[:, None, :].to_broadcast([128, 2, 256]), op=ALU.mult),
                 reads=[Bpb[ob], Brl], writes=[Bon])
            P.op("dve", lambda e: e.scalar_tensor_tensor(out=oh2[k][:, :, j * 128:(j + 1) * 128], in0=onn[:, :, 1, :], scalar=neglam,
                                                         in1=onn[:, :, 0, :], op0=ALU.mult, op1=ALU.add),
                 reads=[Bon, Blam], writes=[Boh2[k]], acc=True)

        def subln_tile(t0, tw):
            xb = S4[sb_i[0] % 4]
            sb_i[0] += 1
            fns = [(lambda e, e2=e2: e.matmul(pb[xb][:, 0:tw], lhsT=ones_b, rhs=sqh[:, e2, t0:t0 + tw],
                                              start=(e2 == 0), stop=(e2 == 1))) for e2 in range(2)]
            P.op("pe", fns, reads=[Bsqh, Bones], writes=[Bpb[xb]])
            P.op("act", lambda e: e.activation(out=rsub[:, t0:t0 + tw], in_=pb[xb][:, 0:tw], func=AF.Sqrt, bias=epsc, scale=1.0 / 256),
                 reads=[Bpb[xb], Bcst], writes=[Brsub])

        def tail(h):
            k = h % 2
            oh = oh2[k]
            P.op("act", lambda e: e.activation(out=sqh, in_=oh, func=AF.Square), reads=[Boh2[k]], writes=[Bsqh])
            for (t0, tw) in tiles_of(NT, 512):
                subln_tile(t0, tw)
            P.op("dve", lambda e: e.reciprocal(out=rsub2, in_=rsub), reads=[Brsub], writes=[Brsub])
            yk = yth.next()
            for e2 in range(2):
                P.op("dve", lambda e, e2=e2: e.scalar_tensor_tensor(out=oh[:, e2, :], in0=oh[:, e2, :], scalar=subc[:, e2:e2 + 1], in1=rsub2,
                                                                  op0=ALU.mult, op1=ALU.mult), reads=[Brsub, Blam], writes=[Boh2[k]])
            P.op("pool", lambda e: e.tensor_tensor(out=oh, in0=oh, in1=hs_gb[k], op=ALU.mult), reads=[Bhs[k]], writes=[Boh2[k]])
            P.op("pool", lambda e: e.tensor_tensor(out=yth.v[yk], in0=oh, in1=yAg2[k], op=ALU.add), reads=[Boh2[k], ByA2[k]], writes=[yth.b[yk]])
            P.op("sp", lambda e: e.dma_start(out=hrow(yT_s, h), in_=yth.v[yk]), reads=[yth.b[yk]], dsem=yth.s[yk])

        def sgu(h):
            for j0 in range(0, NL, 4):
                sgu_group(h, h % 2, j0)

        sgu(0)
        for h in range(H):
            k = h % 2
            for j in range(NL):
                attn_block(h, k, j)
                if j == 0:
                    if h > 0:
                        tail(h - 1)
                    if h + 1 < H:
                        load_head(h + 1)
                if j == NL - 2 and h + 1 < H:
                    sgu(h + 1)
        tail(H - 1)
        P.barrier()

        A.top = PERSIST
        hbuf4 = A.alloc([128, FC, NT], BF16); Bhb4 = Buf("hbuf4")
        wr = Ring(P, A, "w4", 3, [128, FC, 256], BF16)
        xin4 = Ring(P, A, "xin", 8, [128, 512], F32)
        x1s = Ring(P, A, "x1s", 3, [128, 512], F32)
        sq4 = Ring(P, A, "sq4", 2, [128, 512], F32, with_sem=False)
        mtl = [(126 + i * c.FT, c.FT) for i in range(3)]
        Bhb4t = [Buf("hbuf4a"), Buf("hbuf4b"), Buf("hbuf4c")]
        s_hb4 = [P.dsem("hb4a"), P.dsem("hb4b"), P.dsem("hb4c")]
        for ti4, (t04, tw4) in enumerate(mtl):
            P.op("sp", [(lambda e, a=a, t04=t04, tw4=tw4: e.dma_start(out=hbuf4[:, a:a + FC // 2, t04:t04 + tw4],
                                                                   in_=fcv(yT_s)[:, a:a + FC // 2, t04:t04 + tw4]))
                        for a in range(0, FC, FC // 2)], writes=[Bhb4t[ti4]], dsem=s_hb4[ti4])
        assert len(mtl) <= 3

        def epi_out(sp, l, ccg, t0, tw, ps, bb, tg0):
            ti = (t0 - 126) // c.FT
            xi = xin4.next()
            P.op("sp", lambda e: e.dma_start(out=xin4.v[xi][:, 0:tw], in_=xT_loc[ccg * 128:(ccg + 1) * 128, t0:t0 + tw]),
                 writes=[xin4.b[xi]], dsem=xin4.s[xi])
            xs = x1s.next()
            P.op("dve", lambda e: e.scalar_tensor_tensor(out=x1s.v[xs][:, 0:tw], in0=ps[:, 0:tw], scalar=gate1(ccg), in1=xin4.v[xi][:, 0:tw],
                                                         op0=ALU.mult, op1=ALU.add), reads=[bb, xin4.b[xi], Bmod], writes=[x1s.b[xs]])
            q = sq4.next()
            P.op("act", lambda e: e.activation(out=sq4.v[q][:, 0:tw], in_=x1s.v[xs][:, 0:tw], func=AF.Square), reads=[x1s.b[xs]], writes=[sq4.b[q]])
            P.op("sp", lambda e: e.dma_start(out=x1T_s[ccg * 128:(ccg + 1) * 128, t0:t0 + tw], in_=x1s.v[xs][:, 0:tw]),
                 reads=[x1s.b[xs]], dsem=x1s.s[xs])

            def rest():
                P.op("pe", lambda e: e.matmul(pb[4 + ti][:, 0:tw], lhsT=ones_f, rhs=sq4.v[q][:, 0:tw], start=(ccg == 0), stop=(ccg == FC - 1)),
                     reads=[sq4.b[q], Bones], writes=[Bpb[4 + ti]], acc=True)
            pending.append(rest)
        WStream(wr, [(w_out, FC, p * 256, 256, p * 2) for p in range(D // 256)]).run(cons_F(hbuf4, Bhb4t, mtl, epi_out, 0))
        flush_pending()
        P.barrier()
        r2a = A.alloc([128, NT], F32); rs2 = A.alloc([128, NT], F32); Br2 = Buf("r2")
        for ti, (t0, tw) in enumerate(mtl):
            P.op("act", lambda e, ti=ti, t0=t0, tw=tw: e.activation(out=r2a[:, t0:t0 + tw], in_=pb[4 + ti][:, 0:tw], func=AF.Sqrt, bias=epsc, scale=1.0 / D),
                 reads=[Bpb[4 + ti], Bcst], writes=[Br2])
        P.op("dve", lambda e: e.reciprocal(out=rs2, in_=r2a), reads=[Br2], writes=[Br2])
        for fc in range(FC):
            for (t0, tw) in mtl:
                xi = xin4.next()
                P.op("sp", lambda e, xi=xi, fc=fc, t0=t0, tw=tw: e.dma_start(out=xin4.v[xi][:, 0:tw], in_=x1T_s[fc * 128:(fc + 1) * 128, t0:t0 + tw]),
                     writes=[xin4.b[xi]], dsem=xin4.s[xi])
                P.op("dve", lambda e, xi=xi, fc=fc, t0=t0, tw=tw: e.scalar_tensor_tensor(
                    out=xin4.v[xi][:, 0:tw], in0=xin4.v[xi][:, 0:tw], scalar=s2c[:, fc:fc + 1], in1=rs2[:, t0:t0 + tw], op0=ALU.mult, op1=ALU.mult),
                    reads=[Br2, Bmod], writes=[xin4.b[xi]])
                P.op("act", lambda e, xi=xi, fc=fc, t0=t0, tw=tw: e.activation(out=hbuf4[:, fc, t0:t0 + tw], in_=xin4.v[xi][:, 0:tw], func=AF.Identity,
                                                                             bias=shift2(fc), scale=1.0), reads=[xin4.b[xi], Bmod], writes=[Bhb4] + Bhb4t, acc=True)
        P.op("dve", lambda e: e.tensor_scalar_mul(out=hbuf4[:, :, 126:128], in0=hbuf4[:, :, 126:128], scalar1=cc("flag")), reads=[Bcst], writes=[Bhb4] + Bhb4t)
        P.barrier()

        HB_TOP = A.top = PERSIST + (FC * NT + 1) // 2
        wr = Ring(P, A, "w5", 3, [128, FC, 256], BF16)
        FTOK, FT, NOT = c.FTOK, c.FT, c.NOT
        ag = [A.alloc([128, FTOK], F32) for _ in range(2)]; av = [A.alloc([128, FTOK], F32) for _ in range(2)]
        Bag = [Buf("ag0"), Buf("ag1")]; Bav = [Buf("av0"), Buf("av1")]
        cgt = A.alloc([128, NOT], F32); cvt = A.alloc([128, NOT], F32); sgt = A.alloc([128, NOT], F32)
        Bcg, Bcv, Bsg = Buf("cg"), Buf("cv"), Buf("sg")
        ast = Ring(P, A, "ast", 2, [128, NOT], BF16)
        ftiles = [(126 + i * FT, FT) for i in range(3)]
        cw = lambda jj, ch: cc("convw", jj * 2 * FFC + ch)
        cbv = lambda ch: cc("convb", ch)

        def conv(src, dst, Bs, Bd, ch):
            P.op("dve", lambda e: e.tensor_scalar(out=dst, in0=src[:, 0:NOT], scalar1=cw(0, ch), scalar2=cbv(ch), op0=ALU.mult, op1=ALU.add),
                 reads=[Bs, Bcst], writes=[Bd])
            P.op("dve", lambda e: e.scalar_tensor_tensor(out=dst, in0=src[:, 1:NOT + 1], scalar=cw(1, ch), in1=dst, op0=ALU.mult, op1=ALU.add),
                 reads=[Bs, Bcst], writes=[Bd])
            P.op("dve", lambda e: e.scalar_tensor_tensor(out=dst, in0=src[:, 2:NOT + 2], scalar=cw(2, ch), in1=dst, op0=ALU.mult, op1=ALU.add),
                 reads=[Bs, Bcst], writes=[Bd])

        def cons_up(i, sp, wv, wb):
            w_ap, KC, c0, ncols, (isval, c2) = sp
            for l in range(2):
                tgt, Bt = (av[l], Bav[l]) if isval else (ag[l], Bag[l])
                for ti, (t0, tw) in enumerate(ftiles):
                    bk = gemm_bank(6)
                    mm_group(pb[bk][:, 0:tw], [(wv[:, kc, l * 128:(l + 1) * 128], hbuf4[:, kc, t0:t0 + tw]) for kc in range(KC)],
                             [wb, Bhb4], Bpb[bk])
                    P.op("act", lambda e, bk=bk, tgt=tgt, ti=ti, tw=tw: e.activation(out=tgt[:, ti * FT:ti * FT + tw], in_=pb[bk][:, 0:tw], func=AF.Copy),
                         reads=[Bpb[bk]], writes=[Bt])
                if isval:
                    ch = c2 * 2 + l
                    conv(ag[l], cgt, Bag[l], Bcg, ch)
                    conv(av[l], cvt, Bav[l], Bcv, FFC + ch)
                    P.op("act", lambda e: e.activation(out=sgt, in_=cgt, func=AF.Silu), reads=[Bcg], writes=[Bsg])
                    a = ast.next()
                    P.op("dve", lambda e, a=a: e.tensor_tensor(out=ast.v[a], in0=sgt, in1=cvt, op=ALU.mult), reads=[Bsg, Bcv], writes=[ast.b[a]])
                    P.op("sp", lambda e, a=a, ch=ch: e.dma_start(out=actT_s[ch * 128:(ch + 1) * 128, :], in_=ast.v[a]), reads=[ast.b[a]], dsem=ast.s[a])
        specs = []
        g2 = [(w_ada, FC, 5 * D + p * 256, 256, ("ada", p)) for p in range(D // 256)]
        every5 = max(1, (2 * (F // 256)) // (len(g2) + 1))
        for c2 in range(F // 256):
            specs.append((w_up, FC, c2 * 256, 256, (False, c2)))
            specs.append((w_up, FC, F + c2 * 256, 256, (True, c2)))
            if g2 and (len(specs) // 2) % max(1, every5 // 2) == 0:
                specs.append(g2.pop(0))
        specs += g2

        def cons5(i, sp, wv, wb):
            if sp[4][0] == "ada":
                p = sp[4][1]
                for l in range(2):
                    j = p * 2 + l
                    mm_group(pb[7][:, j:j + 1], [(wv[:, kc, l * 128:(l + 1) * 128], scb[:, kc:kc + 1]) for kc in range(FC)], [wb, Bsc], Bpb[7])
            else:
                cons_up(i, sp, wv, wb)
        WStream(wr, specs).run(cons5)
        P.op("dve", lambda e: e.tensor_tensor(out=modT[:, 5 * FC:6 * FC], in0=pb[7][:, 0:FC], in1=cc("bada", 5 * FC, FC), op=ALU.add),
             reads=[Bpb[7], Bcst], writes=[Bmod])
        P.barrier()

        A.top = PERSIST_SMALL
        DH = c.DH
        NTT = c.NOT // DH
        KG = 2
        NCC = min(8 // NTT, FC)
        CGW = NCC * 128
        hbuf6 = A.alloc([128, FFC, c.NOT], BF16); Bhb6 = Buf("hbuf6"); s_hb6 = P.dsem("hb6")
        wr6 = Ring(P, A, "w6", 4, [128, KG, CGW], BF16)
        xin6 = Ring(P, A, "xin6", 8, [128, DH], F32)
        ost = Ring(P, A, "ost", 3, [128, DH], F32)
        outs = []

        xpre = {}

        def dn_pre(ccg, hf):
            xi = xin6.next()
            P.op("sp", lambda e: e.dma_start(out=xin6.v[xi], in_=x1T_s[ccg * 128:(ccg + 1) * 128, 128 + hf * DH:128 + (hf + 1) * DH]),
                 writes=[xin6.b[xi]], dsem=xin6.s[xi])
            xpre[(ccg, hf)] = xi

        def dn_epi(ccg, bk, hf):
            xi = xpre.pop((ccg, hf))
            o = ost.next()
            P.op("dve", lambda e: e.scalar_tensor_tensor(out=ost.v[o], in0=pb[bk][:, 0:DH], scalar=gate2(ccg), in1=xin6.v[xi],
                                                         op0=ALU.mult, op1=ALU.add), reads=[Bpb[bk], xin6.b[xi], Bmod], writes=[ost.b[o]])
            outs.append(P.op("sp", lambda e: e.dma_start(out=outT[ccg * 128:(ccg + 1) * 128, hf * DH:(hf + 1) * DH], in_=ost.v[o]),
                             reads=[ost.b[o]], dsem=ost.s[o]))

        NKG = FFC // KG
        g6 = max(KG, ((FFC // 8 + KG - 1) // KG) * KG)
        Bhb6g = []
        for gi, a in enumerate(range(0, FFC, g6)):
            bgi = Buf(f"hbuf6_{gi}")
            Bhb6g.append(bgi)
            P.op("sp", lambda e, a=a: e.dma_start(out=hbuf6[:, a:min(FFC, a + g6), :],
                                                  in_=actT_s.rearrange("(kc p) t -> p kc t", p=128)[:, a:min(FFC, a + g6), :]),
                 writes=[bgi], dsem=P.dsem(f"hb6_{gi}"))

        def cons_dn(i, sp, wv, wb):
            w_ap, KC, c0, ncols, (cg, kg) = sp
            if kg == 0:
                for l in range(NCC):
                    for tt in range(NTT):
                        dn_pre(cg * NCC + l, tt)
            fns = []
            for kcl in range(KG):
                for l in range(NCC):
                    for tt in range(NTT):
                        first = (kg == 0 and kcl == 0)
                        last = (kg == NKG - 1 and kcl == KG - 1)
                        fns.append(lambda e, kcl=kcl, l=l, tt=tt, first=first, last=last: e.matmul(
                            pb[l * NTT + tt][:, 0:DH], lhsT=wv[:, kcl, l * 128:(l + 1) * 128],
                            rhs=hbuf6[:, kg * KG + kcl, tt * DH:(tt + 1) * DH], start=first, stop=last))
            P.op("pe", fns, reads=[wb, Bhb6g[(kg * KG) // g6]], writes=[Bpb[i] for i in range(NCC * NTT)], acc=(kg > 0))
            if kg == NKG - 1:
                for l in range(NCC):
                    for tt in range(NTT):
                        dn_epi(cg * NCC + l, l * NTT + tt, tt)
        specs = []
        for cg in range(D // CGW):
            for kg in range(NKG):
                specs.append((w_down[kg * KG * 128:(kg + 1) * KG * 128, :], KG, cg * CGW, CGW, (cg, kg)))
        WStream(wr6, specs).run(cons_dn)
        P.run(outs)
    return nc


def host_inputs(cfg, inp, core):
    c = cfg
    b, half = core // 2, core % 2
    D, H, S, F, FC, NB, NO, NL, NT, FFC = c.D, c.H, c.S, c.F, c.FC, c.NB, c.NO, c.NL, c.NT, c.FFC
    f32 = np.float32
    x = inp["x"][b]
    xT = np.ascontiguousarray(x.T)
    if half == 1:
        loc = xT[:, (NO - 1) * 128:]
    else:
        loc = np.concatenate([xT[:, 0:128], xT[:, 0:NO * 128]], axis=1)
    colT = lambda v: np.ascontiguousarray(np.asarray(v, f32).reshape(-1, 128).T)
    cst = np.zeros((128, c.NCST), f32)

    def put(name, arr):
        o, w = c.coff[name]
        arr = np.asarray(arr, f32)
        assert arr.shape == (128, w), (name, arr.shape, w)
        cst[:, o:o + w] = arr
    put("cT", colT(inp["c"][b]))
    put("bada", colT(inp["b_ada"][0]))
    put("n1g", colT(inp["norm1_g"][0])); put("n2g", colT(inp["norm2_g"][0])); put("sgug", colT(inp["sgu_norm_g"][0]))
    put("qg", colT(inp["q_norm_g"][0])); put("kg", colT(inp["k_norm_g"][0]))
    put("lam", np.stack([inp["lambda_q1"][0], inp["lambda_q2"][0], inp["lambda_k1"][0], inp["lambda_k2"][0]], axis=1))
    put("subg", colT(inp["subln_g"][0]))
    cw = np.asarray(inp["conv_w"][0], f32)
    put("convw", np.concatenate([colT(cw[j]) for j in range(3)], axis=1))
    put("convb", colT(inp["conv_b"][0]))
    put("flag", np.full((128, 1), float(half), f32))
    put("eps", np.full((128, 1), EPS, f32))
    slopes = 2.0 ** (-8.0 * np.arange(1, H + 1, dtype=np.float64) / H)
    cb = np.full((H, NL, NB), NEG, np.float64)
    for j in range(NL):
        gq = (NO - 1 + j) if half == 1 else max(j - 1, 0)
        for kb in range(NB):
            if kb <= gq:
                cb[:, j, kb] = -slopes * 128.0 * (gq - kb) - M0
    put("cb", np.broadcast_to(cb.reshape(1, -1).astype(f32), (128, H * NL * NB)))
    kk = np.arange(128)[:, None]; qq = np.arange(128)[None, :]
    put("tri", (kk <= qq).astype(f32))
    bsp = np.asarray(inp["b_spatial"][0], f32)
    put("bsp", np.broadcast_to(bsp.reshape(1, -1), (128, c.G * 128)))
    ws = np.asarray(inp["w_spatial"][0], f32)
    put("wsT", np.ascontiguousarray(ws.transpose(2, 0, 1)).reshape(128, c.G * 128))
    import ml_dtypes
    bf = ml_dtypes.bfloat16
    SC = 128.0 ** -0.5
    abt = np.zeros((H, 4, 384), np.float32)
    kidx = np.arange(128, dtype=np.float32)
    for h in range(H):
        sg = slopes[h] / SC
        sh = np.float32(np.float32(sg).astype(bf))
        sl = np.float32(np.float32(sg - float(sh)).astype(bf))
        abt[h, 0, 0:128] = kidx; abt[h, 1, 0:128] = kidx; abt[h, 2, 0:128] = -sh; abt[h, 3, 0:128] = -sl
        abt[h, 0, 128:384] = sh; abt[h, 1, 128:384] = sl
        abt[h, 2, 128:384] = np.tile(kidx, 2); abt[h, 3, 128:384] = np.tile(kidx, 2)
    msk = np.zeros((128, 3, 256), np.float32)
    msk[:, 0, 0:128] = np.eye(128, dtype=np.float32)
    tri = np.where(qq >= kk, 0.0, NEG / SC).astype(np.float32)
    tri2 = np.concatenate([tri, tri], axis=1)
    if half == 1:
        msk[:, 1, :] = tri2
    else:
        msk[:, 2, :] = tri2
    return {
        "xT_ctx": xT, "xT_loc": np.ascontiguousarray(loc), "cst": cst, "abt": abt.astype(bf), "msk": msk.astype(bf),
        "w_ada": np.asarray(inp["w_ada"][0]), "w_in": np.asarray(inp["w_in"][0]), "w_out": np.asarray(inp["w_out"][0]),
        "w_up": np.asarray(inp["w_ff_up"][0]), "w_down": np.asarray(inp["w_ff_down"][0]),
    }


def assemble(cfg, results):
    c = cfg
    out = np.zeros((c.B, c.S, c.D), np.float32)
    for core, r in enumerate(results):
        b, half = core // 2, core % 2
        out[b, half * c.NOT:(half + 1) * c.NOT, :] = r["outT"].T
    return out


_NC_CACHE = {}


def kernel(**inputs):
    cfg = Cfg()
    inp = {k: np.asarray(v) for k, v in inputs.items()}
    if "nc" not in _NC_CACHE:
        _NC_CACHE["nc"] = build(cfg)
    nc = _NC_CACHE["nc"]
    in_maps = [host_inputs(cfg, inp, core) for core in range(8)]
    res = run_bass_kernel_spmd(nc, in_maps, core_ids=list(range(8)))
    return assemble(cfg, res.results)
```
